# Optimizing a Trainium2 kernel written in Bass

```python
import math
import jax
import jax.numpy as jnp
from jax import lax
import numpy as np


D_MODEL = 1024
BATCH = 1
SEQ = 16384
DEPTH = 2

CHUNK = 64
Q_BLOCK = 128
N_MEM = 256
HEAD_DIM = 64
EPS = 1e-6
A_HEADS = D_MODEL // (2 * HEAD_DIM)
IDX_HEADS = 8
IDX_DIM = 64
TOPK_MAX = 256
B_VDIM = 2 * HEAD_DIM
B_HEADS = D_MODEL // (2 * B_VDIM)
C_HEADS = D_MODEL // HEAD_DIM
C_BAND = 9
REL_CLIP = 256
T5_BUCKETS = 32
T5_MAX_DIST = 128
T5_HEADS = A_HEADS + B_HEADS
M_HEADS = 4
M_DIM = 128
D_FF = 2816

kernel_name = 'hybrid_chunk_causal_encoder'


def even_split_sizes():
    return [A_HEADS * HEAD_DIM, A_HEADS * HEAD_DIM, A_HEADS * HEAD_DIM,
            IDX_HEADS * IDX_DIM, IDX_DIM, IDX_HEADS,
            B_HEADS * 2 * HEAD_DIM, B_HEADS * 2 * HEAD_DIM, B_HEADS * B_VDIM]


def rmsnorm(x, g):
    xf = x.astype(jnp.float32)
    y = xf * lax.rsqrt(jnp.mean(xf * xf, axis=-1, keepdims=True) + EPS)
    return (y * g.astype(jnp.float32)).astype(x.dtype)


def swiglu(x, wg, wu, wd):
    return (jax.nn.silu(x @ wg) * (x @ wu)) @ wd


def t5_bucket(rel):
    nb = T5_BUCKETS // 2
    max_exact = nb // 2
    offset = (rel < 0).astype(jnp.int32) * nb
    n = jnp.abs(rel)
    nf = jnp.maximum(n, 1).astype(jnp.float32)
    large = max_exact + (jnp.log(nf / max_exact) / math.log(T5_MAX_DIST / max_exact)
                         * (nb - max_exact)).astype(jnp.int32)
    large = jnp.minimum(large, nb - 1)
    return offset + jnp.where(n < max_exact, n, large)


def dsa_attention(q, k, v, iq, ik, iw, bias_table):
    bsz, seq, heads, dh = q.shape
    top_k = min(TOPK_MAX, seq // 4)
    key_chunk = jnp.arange(seq, dtype=jnp.int32) // CHUNK
    bias_table = bias_table.astype(jnp.float32)

    def block(i):
        start = i * Q_BLOCK
        qpos = start + jnp.arange(Q_BLOCK, dtype=jnp.int32)
        qchunk = qpos // CHUNK
        qb = lax.dynamic_slice_in_dim(q, start, Q_BLOCK, axis=1)
        iqb = lax.dynamic_slice_in_dim(iq, start, Q_BLOCK, axis=1)
        iwb = lax.dynamic_slice_in_dim(iw, start, Q_BLOCK, axis=1).astype(jnp.float32) * IDX_HEADS ** -0.5
        idx_logits = jnp.einsum('bqhd,bsd->bqhs', iqb, ik).astype(jnp.float32) * IDX_DIM ** -0.5
        score = jnp.einsum('bqh,bqhs->bqs', iwb, jax.nn.relu(idx_logits))
        admissible = key_chunk[None, :] <= qchunk[:, None]
        score = jnp.where(admissible[None], score, -jnp.inf)
        _, sel = lax.top_k(score, top_k)
        valid = (sel // CHUNK) <= qchunk[None, :, None]
        kg = jax.vmap(lambda kk, ii: kk[ii])(k, sel)
        vg = jax.vmap(lambda vv, ii: vv[ii])(v, sel)
        logits = jnp.einsum('bqhd,bqkhd->bhqk', qb, kg).astype(jnp.float32) * dh ** -0.5
        bias = bias_table[t5_bucket(qpos[None, :, None] - sel)]
        logits = logits + jnp.transpose(bias, (0, 3, 1, 2))
        logits = jnp.where(valid[:, None], logits, -jnp.inf)
        p = jax.nn.softmax(logits, axis=-1).astype(v.dtype)
        return jnp.einsum('bhqk,bqkhd->bqhd', p, vg)

    out = lax.map(block, jnp.arange(seq // Q_BLOCK, dtype=jnp.int32))
    return jnp.transpose(out, (1, 0, 2, 3, 4)).reshape(bsz, seq, heads * dh)


def diff_attention(q, k, v, lam, subln_g, bias_table):
    bsz, seq, heads, _, dh = q.shape
    pos = jnp.arange(seq, dtype=jnp.int32)
    key_chunk = pos // CHUNK
    bias_table = bias_table.astype(jnp.float32)

    def block(i):
        start = i * Q_BLOCK
        qpos = start + jnp.arange(Q_BLOCK, dtype=jnp.int32)
        qb = lax.dynamic_slice_in_dim(q, start, Q_BLOCK, axis=1)
        logits = jnp.einsum('bqhmd,bshmd->bhmqs', qb, k).astype(jnp.float32) * dh ** -0.5
        bias = bias_table[t5_bucket(qpos[:, None] - pos[None, :])]
        logits = logits + jnp.transpose(bias, (2, 0, 1))[None, :, None]
        mask = key_chunk[None, :] <= (qpos // CHUNK)[:, None]
        logits = jnp.where(mask, logits, -jnp.inf)
        p = jax.nn.softmax(logits, axis=-1)
        attn = (p[:, :, 0] - lam * p[:, :, 1]).astype(v.dtype)
        o = jnp.einsum('bhqs,bshe->bqhe', attn, v)
        return rmsnorm(o, subln_g)

    out = lax.map(block, jnp.arange(seq // Q_BLOCK, dtype=jnp.int32))
    return jnp.transpose(out, (1, 0, 2, 3, 4)).reshape(bsz, seq, heads * v.shape[-1])


def chunk_band_attention(q, k, v, rel_bias):
    bsz, seq, heads, dh = q.shape
    pad = (C_BAND - 1) * CHUNK
    band = C_BAND * CHUNK
    kp = jnp.pad(k, ((0, 0), (pad, 0), (0, 0), (0, 0)))
    vp = jnp.pad(v, ((0, 0), (pad, 0), (0, 0), (0, 0)))
    qoff = jnp.arange(CHUNK, dtype=jnp.int32)
    koff = jnp.arange(band, dtype=jnp.int32)
    rel = (qoff[:, None] + pad) - koff[None, :]
    rel_idx = jnp.clip(rel, -REL_CLIP, REL_CLIP) + REL_CLIP
    bias = jnp.transpose(rel_bias.astype(jnp.float32)[rel_idx], (2, 0, 1))

    def one_chunk(c):
        qc = lax.dynamic_slice_in_dim(q, c * CHUNK, CHUNK, axis=1)
        kc = lax.dynamic_slice_in_dim(kp, c * CHUNK, band, axis=1)
        vc = lax.dynamic_slice_in_dim(vp, c * CHUNK, band, axis=1)
        valid = (c * CHUNK - pad + koff) >= 0
        logits = jnp.einsum('bqhd,bkhd->bhqk', qc, kc).astype(jnp.float32) * dh ** -0.5 + bias[None]
        logits = jnp.where(valid, logits, -jnp.inf)
        p = jax.nn.softmax(logits, axis=-1).astype(vc.dtype)
        return jnp.einsum('bhqk,bkhd->bqhd', p, vc)

    out = lax.map(one_chunk, jnp.arange(seq // CHUNK, dtype=jnp.int32))
    return jnp.transpose(out, (1, 0, 2, 3, 4)).reshape(bsz, seq, heads * dh)


def even_mixer(h, w_in, a_qg, a_kg, idx_kg, b_qg, b_kg, lq1, lk1, lq2, lk2, b_subln, w_out, t5_bias, layer_idx):
    bsz, seq, _ = h.shape
    cuts = [int(c) for c in np.cumsum(even_split_sizes())[:-1]]
    aq, ak, av, iq, ik, iw, bq, bk, bv = jnp.split(h @ w_in, cuts, axis=-1)
    aq = rmsnorm(aq.reshape(bsz, seq, A_HEADS, HEAD_DIM), a_qg)
    ak = rmsnorm(ak.reshape(bsz, seq, A_HEADS, HEAD_DIM), a_kg)
    av = av.reshape(bsz, seq, A_HEADS, HEAD_DIM)
    iq = iq.reshape(bsz, seq, IDX_HEADS, IDX_DIM)
    ik = rmsnorm(ik, idx_kg)
    out_a = dsa_attention(aq, ak, av, iq, ik, iw, t5_bias[:, :A_HEADS])
    bq = rmsnorm(bq.reshape(bsz, seq, B_HEADS, 2, HEAD_DIM), b_qg)
    bk = rmsnorm(bk.reshape(bsz, seq, B_HEADS, 2, HEAD_DIM), b_kg)
    bv = bv.reshape(bsz, seq, B_HEADS, B_VDIM)
    lambda_init = 0.8 - 0.6 * math.exp(-0.3 * layer_idx)
    lam = (jnp.exp(jnp.sum(lq1.astype(jnp.float32) * lk1.astype(jnp.float32)))
           - jnp.exp(jnp.sum(lq2.astype(jnp.float32) * lk2.astype(jnp.float32))) + lambda_init)
    out_b = diff_attention(bq, bk, bv, lam, b_subln, t5_bias[:, A_HEADS:]) * (1.0 - lambda_init)
    return jnp.concatenate([out_a, out_b], axis=-1) @ w_out


def odd_mixer(h, w_in, c_qg, c_kg, rel_bias, w_out):
    bsz, seq, _ = h.shape
    q, k, v = jnp.split(h @ w_in, 3, axis=-1)
    q = rmsnorm(q.reshape(bsz, seq, C_HEADS, HEAD_DIM), c_qg)
    k = rmsnorm(k.reshape(bsz, seq, C_HEADS, HEAD_DIM), c_kg)
    v = v.reshape(bsz, seq, C_HEADS, HEAD_DIM)
    return chunk_band_attention(q, k, v, rel_bias) @ w_out


def memory_xattn(h, m, wq, wkv, qg, kg, wo):
    bsz, seq, _ = h.shape
    q = rmsnorm((h @ wq).reshape(bsz, seq, M_HEADS, M_DIM), qg)
    k, v = jnp.split(m @ wkv, 2, axis=-1)
    k = rmsnorm(k.reshape(bsz, -1, M_HEADS, M_DIM), kg)
    v = v.reshape(bsz, -1, M_HEADS, M_DIM)
    logits = jnp.einsum('bshd,bmhd->bhsm', q, k).astype(jnp.float32) * M_DIM ** -0.5
    p = jax.nn.softmax(logits, axis=-1).astype(v.dtype)
    o = jnp.einsum('bhsm,bmhd->bshd', p, v).reshape(bsz, seq, M_HEADS * M_DIM)
    return o @ wo


def setup_inputs(seed: int = 0) -> dict:
    key = jax.random.key(seed)
    counter = [0]

    def nk():
        counter[0] += 1
        return jax.random.fold_in(key, counter[0])

    def dense(fan_in, fan_out):
        return jax.random.normal(nk(), (fan_in, fan_out), jnp.float32) * fan_in ** -0.5

    def gain(n):
        return 1.0 + 0.02 * jax.random.normal(nk(), (n,), jnp.float32)

    def small(shape, scale):
        return scale * jax.random.normal(nk(), shape, jnp.float32)

    p = {}
    p['x'] = jax.random.normal(nk(), (BATCH, SEQ, D_MODEL), jnp.float32)
    p['mem'] = jax.random.normal(nk(), (BATCH, N_MEM, D_MODEL), jnp.float32)
    p['t5_bias'] = small((T5_BUCKETS, T5_HEADS), 0.2)
    for layer in range(DEPTH):
        pre = 'l%d_' % layer
        p[pre + 'ffn1_norm'] = gain(D_MODEL)
        p[pre + 'ffn1_wg'] = dense(D_MODEL, D_FF)
        p[pre + 'ffn1_wu'] = dense(D_MODEL, D_FF)
        p[pre + 'ffn1_wd'] = dense(D_FF, D_MODEL)
        p[pre + 'mix_norm'] = gain(D_MODEL)
        if layer % 2 == 0:
            p[pre + 'w_in'] = dense(D_MODEL, sum(even_split_sizes()))
            p[pre + 'a_q_norm'] = gain(HEAD_DIM)
            p[pre + 'a_k_norm'] = gain(HEAD_DIM)
            p[pre + 'idx_k_norm'] = gain(IDX_DIM)
            p[pre + 'b_q_norm'] = gain(HEAD_DIM)
            p[pre + 'b_k_norm'] = gain(HEAD_DIM)
            p[pre + 'b_lq1'] = small((HEAD_DIM,), 0.1)
            p[pre + 'b_lk1'] = small((HEAD_DIM,), 0.1)
            p[pre + 'b_lq2'] = small((HEAD_DIM,), 0.1)
            p[pre + 'b_lk2'] = small((HEAD_DIM,), 0.1)
            p[pre + 'b_subln'] = gain(B_VDIM)
            p[pre + 'w_out'] = dense(A_HEADS * HEAD_DIM + B_HEADS * B_VDIM, D_MODEL)
        else:
            p[pre + 'w_in'] = dense(D_MODEL, 3 * C_HEADS * HEAD_DIM)
            p[pre + 'c_q_norm'] = gain(HEAD_DIM)
            p[pre + 'c_k_norm'] = gain(HEAD_DIM)
            p[pre + 'c_rel_bias'] = small((2 * REL_CLIP + 1, C_HEADS), 0.2)
            p[pre + 'w_out'] = dense(C_HEADS * HEAD_DIM, D_MODEL)
        p[pre + 'mem_norm'] = gain(D_MODEL)
        p[pre + 'mem_src_norm'] = gain(D_MODEL)
        p[pre + 'mem_wq'] = dense(D_MODEL, M_HEADS * M_DIM)
        p[pre + 'mem_wkv'] = dense(D_MODEL, 2 * M_HEADS * M_DIM)
        p[pre + 'mem_q_norm'] = gain(M_DIM)
        p[pre + 'mem_k_norm'] = gain(M_DIM)
        p[pre + 'mem_wo'] = dense(M_HEADS * M_DIM, D_MODEL)
        p[pre + 'ffn2_norm'] = gain(D_MODEL)
        p[pre + 'ffn2_wg'] = dense(D_MODEL, D_FF)
        p[pre + 'ffn2_wu'] = dense(D_MODEL, D_FF)
        p[pre + 'ffn2_wd'] = dense(D_FF, D_MODEL)
    return p


def reference(x, mem, t5_bias,
              l0_ffn1_norm, l0_ffn1_wg, l0_ffn1_wu, l0_ffn1_wd,
              l0_mix_norm, l0_w_in, l0_a_q_norm, l0_a_k_norm, l0_idx_k_norm,
              l0_b_q_norm, l0_b_k_norm, l0_b_lq1, l0_b_lk1, l0_b_lq2, l0_b_lk2, l0_b_subln, l0_w_out,
              l0_mem_norm, l0_mem_src_norm, l0_mem_wq, l0_mem_wkv, l0_mem_q_norm, l0_mem_k_norm, l0_mem_wo,
              l0_ffn2_norm, l0_ffn2_wg, l0_ffn2_wu, l0_ffn2_wd,
              l1_ffn1_norm, l1_ffn1_wg, l1_ffn1_wu, l1_ffn1_wd,
              l1_mix_norm, l1_w_in, l1_c_q_norm, l1_c_k_norm, l1_c_rel_bias, l1_w_out,
              l1_mem_norm, l1_mem_src_norm, l1_mem_wq, l1_mem_wkv, l1_mem_q_norm, l1_mem_k_norm, l1_mem_wo,
              l1_ffn2_norm, l1_ffn2_wg, l1_ffn2_wu, l1_ffn2_wd):
    ffn1 = [(l0_ffn1_norm, l0_ffn1_wg, l0_ffn1_wu, l0_ffn1_wd),
            (l1_ffn1_norm, l1_ffn1_wg, l1_ffn1_wu, l1_ffn1_wd)]
    mix = [(l0_mix_norm, (l0_w_in, l0_a_q_norm, l0_a_k_norm, l0_idx_k_norm, l0_b_q_norm, l0_b_k_norm,
                          l0_b_lq1, l0_b_lk1, l0_b_lq2, l0_b_lk2, l0_b_subln, l0_w_out)),
           (l1_mix_norm, (l1_w_in, l1_c_q_norm, l1_c_k_norm, l1_c_rel_bias, l1_w_out))]
    memx = [(l0_mem_norm, l0_mem_src_norm, l0_mem_wq, l0_mem_wkv, l0_mem_q_norm, l0_mem_k_norm, l0_mem_wo),
            (l1_mem_norm, l1_mem_src_norm, l1_mem_wq, l1_mem_wkv, l1_mem_q_norm, l1_mem_k_norm, l1_mem_wo)]
    ffn2 = [(l0_ffn2_norm, l0_ffn2_wg, l0_ffn2_wu, l0_ffn2_wd),
            (l1_ffn2_norm, l1_ffn2_wg, l1_ffn2_wu, l1_ffn2_wd)]
    h = x
    for layer in range(DEPTH):
        g, wg, wu, wd = ffn1[layer]
        h = h + 0.5 * swiglu(rmsnorm(h, g), wg, wu, wd)
        mix_g, mix_params = mix[layer]
        if layer % 2 == 0:
            h = h + even_mixer(rmsnorm(h, mix_g), *mix_params, t5_bias, layer)
        else:
            h = h + odd_mixer(rmsnorm(h, mix_g), *mix_params)
        mg, sg, wq, wkv, qg, kg, wo = memx[layer]
        h = h + memory_xattn(rmsnorm(h, mg), rmsnorm(mem, sg), wq, wkv, qg, kg, wo)
        g, wg, wu, wd = ffn2[layer]
        h = h + 0.5 * swiglu(rmsnorm(h, g), wg, wu, wd)
    return h
```

```python
import math
import numpy as np
from contextlib import ExitStack, contextmanager
import concourse.bass as bass
import concourse.mybir as mybir
from concourse.bass_utils import run_bass_kernel_spmd

F32 = mybir.dt.float32
BF16 = mybir.dt.bfloat16
AF = mybir.ActivationFunctionType
ALU = mybir.AluOpType


STRICT = True


class Res:
    __slots__ = ("name", "lw", "lw_eng", "rd")

    def __init__(self, name=""):
        self.name = name
        self.lw = None
        self.lw_eng = None
        self.rd = []


class Eng:
    def __init__(self, name, h, sem, is_pe=False):
        self.name = name
        self.h = h
        self.sem = sem
        self.n = 0
        self.seen = {}
        self.is_pe = is_pe
        self.nwaits = 0
        self.ninst = 0


class Sched:
    def __init__(self, nc, es, n_dma_sems=32):
        self.nc = nc
        self.E = {}
        for name, h in (("pe", nc.tensor), ("act", nc.scalar), ("dve", nc.vector),
                        ("pool", nc.gpsimd), ("sp", nc.sync)):
            sem = es.enter_context(nc.semaphore("c_" + name))
            self.E[name] = Eng(name, h, sem, is_pe=(name == "pe"))
        self.dsems = [es.enter_context(nc.semaphore("d%d" % i)) for i in range(n_dma_sems)]
        self.dcnt = [0] * n_dma_sems
        self.dpool = {"pool": list(range(0, 8)), "sp": list(range(8, n_dma_sems)), "act": list(range(8, n_dma_sems))}
        self.dnext = {"pool": 0, "sp": 0, "act": 0}

    def _deps(self, eng, reads, writes):
        deps = []
        pe_same = (eng is not None and eng.is_pe)
        for r in reads:
            if r.lw is not None:
                if r.lw_eng is eng and pe_same:
                    continue
                deps.append(r.lw)
        for w in writes:
            if w.lw is not None and (STRICT or w.lw_eng is not eng) and not (pe_same and w.lw_eng is eng):
                deps.append(w.lw)
            for ev, e in w.rd:
                if (STRICT or e is not eng) and not (pe_same and e is eng):
                    deps.append(ev)
        return deps

    def _wait(self, eng, deps):
        best = {}
        for key, sem, val in deps:
            if eng.seen.get(key, 0) >= val:
                continue
            if key not in best or best[key][1] < val:
                best[key] = (sem, val)
        for key, (sem, val) in best.items():
            eng.h.wait_ge(sem, val)
            eng.seen[key] = val
            eng.nwaits += 1

    def op(self, ename, fn, reads=(), writes=()):
        eng = self.E[ename]
        self._wait(eng, self._deps(eng, reads, writes))
        inst = fn(eng.h)
        eng.n += 1
        eng.ninst += 1
        inst.then_inc(eng.sem, 1)
        ev = (ename, eng.sem, eng.n)
        for r in reads:
            r.rd.append((ev, eng))
        for w in writes:
            w.lw = ev
            w.lw_eng = eng
            w.rd = []
        return ev

    def dma(self, qname, out, in_, reads=(), writes=()):
        eng = self.E[qname]
        self._wait(eng, self._deps(None, reads, writes))
        lst = self.dpool[qname]
        i = lst[self.dnext[qname] % len(lst)]
        self.dnext[qname] += 1
        inst = eng.h.dma_start(out=out, in_=in_)
        self.dcnt[i] += 16
        inst.then_inc(self.dsems[i], 16)
        eng.ninst += 1
        ev = ("d%d" % i, self.dsems[i], self.dcnt[i])
        for r in reads:
            r.rd.append((ev, None))
        for w in writes:
            w.lw = ev
            w.lw_eng = None
            w.rd = []
        return ev

    def wait_event(self, ename, ev):
        self._wait(self.E[ename], [ev])

    def finish(self, ename, resources):
        eng = self.E[ename]
        deps = [r.lw for r in resources if r.lw is not None]
        self._wait(eng, deps)


def barrier(S):
    for e in S.E.values():
        deps = []
        for f in S.E.values():
            if f is not e and f.n > 0:
                deps.append((f.name, f.sem, f.n))
        for i, s in enumerate(S.dsems):
            if S.dcnt[i] > 0:
                deps.append(("d%d" % i, s, S.dcnt[i]))
        S._wait(e, deps)


Sched.barrier = barrier


EPS = 1e-6
FFG = [(0, 4), (4, 8), (8, 12), (12, 16), (16, 19), (19, 22)]


class Ctx:
    def __init__(self, nc, es, TOK):
        self.nc = nc
        self.es = es
        self.TOK = TOK
        self.uid = 0
        self.stack = [es]
        self.NTG = TOK // 512
        self.S = Sched(nc, es)
        self.psall = es.enter_context(nc.psum_tensor("psall", [128, 8, 512], F32))
        self.ps = [self.psall[:, i, :] for i in range(8)]
        self.psr = [Res("ps%d" % i) for i in range(8)]
        self.ones_bf = es.enter_context(nc.sbuf_tensor("ones_bf", [128, 128], BF16))
        self.ones_r = Res("ones")
        self.S.op("pool", lambda e: e.memset(self.ones_bf[:], 1.0), writes=[self.ones_r])

    def sb(self, name, shape, dt):
        self.uid += 1
        return self.stack[-1].enter_context(self.nc.sbuf_tensor("%s_%d" % (name, self.uid), shape, dt))

    @contextmanager
    def scope(self):
        with ExitStack() as st:
            self.stack.append(st)
            try:
                yield
            finally:
                self.S.barrier()
                self.stack.pop()


def rmsnorm_fm(C, hT, hres, g_ap, g_res, xnT, xres, tmp):
    S = C.S
    for tg in range(C.NTG):
        sl = slice(tg * 512, (tg + 1) * 512)
        pb = 6 + (tg % 2)
        for kc in range(8):
            b = kc % 2
            S.op("act", lambda e: e.activation(out=tmp["sq"][b][:], in_=hT[:, kc, sl], func=AF.Square),
                 reads=[hres[kc][tg]], writes=[tmp["sq_r"][b]])
            S.op("pe", lambda e: e.matmul(C.ps[pb][:], lhsT=C.ones_bf[:], rhs=tmp["sq"][b][:],
                                          start=(kc == 0), stop=(kc == 7)),
                 reads=[tmp["sq_r"][b], C.ones_r], writes=[C.psr[pb]])
        S.op("act", lambda e: e.activation(out=tmp["rt"][:], in_=C.ps[pb][:], func=AF.Sqrt,
                                           scale=1.0 / 1024.0, bias=tmp["eps"][:]),
             reads=[C.psr[pb], tmp["eps_r"]], writes=[tmp["rt_r"]])
        S.op("dve", lambda e: e.reciprocal(out=tmp["rstd"][:], in_=tmp["rt"][:]),
             reads=[tmp["rt_r"]], writes=[tmp["rstd_r"]])
        for kc in range(8):
            S.op("dve", lambda e: e.scalar_tensor_tensor(out=xnT[:, kc, sl], in0=hT[:, kc, sl],
                                                         scalar=g_ap[:, kc:kc + 1], in1=tmp["rstd"][:],
                                                         op0=ALU.mult, op1=ALU.mult),
                 reads=[hres[kc][tg], g_res, tmp["rstd_r"]], writes=[xres[kc][tg]])


def make_norm_tmp(C):
    t = {}
    t["sq"] = [C.sb("nsq%d" % i, [128, 512], BF16) for i in range(2)]
    t["sq_r"] = [Res() for _ in range(2)]
    t["rt"] = C.sb("nrt", [128, 512], F32)
    t["rt_r"] = Res()
    t["rstd"] = C.sb("nrstd", [128, 512], F32)
    t["rstd_r"] = Res()
    t["eps"] = C.sb("neps", [128, 1], F32)
    t["eps_r"] = Res()
    C.S.op("pool", lambda e: e.memset(t["eps"][:], EPS), writes=[t["eps_r"]])
    return t


def ffn_fm(C, hT, hres, xnT, xres, wgu_d, wd_d, bufs):
    S = C.S
    NTG = C.NTG
    wd_v = wd_d.rearrange("(c p) d -> p c d", p=128)
    cnt = bufs["cnt"]
    for gi, (c0, c1) in enumerate(FFG):
        gp = cnt["g"] % 2
        cnt["g"] += 1
        nch = c1 - c0
        S.dma("pool", bufs["wd"][gp][:, 0:nch, :], wd_v[:, c0:c1, :], writes=[bufs["wd_r"][gp]])
        for ffc in range(c0, c1):
            l = ffc - c0
            wb = cnt["w"] % 3
            cnt["w"] += 1
            S.dma("pool", bufs["wgu"][wb][:], wgu_d[ffc], writes=[bufs["wgu_r"][wb]])
            for tg in range(NTG):
                sl = slice(tg * 512, (tg + 1) * 512)
                pp = cnt["p"] % 2
                cnt["p"] += 1
                pg, pu = 2 * pp, 2 * pp + 1
                for which, pbank in ((0, pg), (1, pu)):
                    for kc in range(8):
                        S.op("pe", lambda e: e.matmul(C.ps[pbank][:], lhsT=bufs["wgu"][wb][:, which, kc, :],
                                                      rhs=xnT[:, kc, sl], start=(kc == 0), stop=(kc == 7)),
                             reads=[bufs["wgu_r"][wb], xres[kc][tg]], writes=[C.psr[pbank]])
                sb_ = cnt["s"] % 2
                cnt["s"] += 1
                S.op("act", lambda e: e.activation(out=bufs["sg"][sb_][:], in_=C.ps[pg][:], func=AF.Silu),
                     reads=[C.psr[pg]], writes=[bufs["sg_r"][sb_]])
                S.op("dve", lambda e: e.tensor_tensor(out=bufs["act"][gp][:, l, sl], in0=C.ps[pu][:],
                                                      in1=bufs["sg"][sb_][:], op=ALU.mult),
                     reads=[C.psr[pu], bufs["sg_r"][sb_]], writes=[bufs["act_r"][gp][l][tg]])
        for tg in range(NTG):
            sl = slice(tg * 512, (tg + 1) * 512)
            for dmc in range(8):
                pd = 4 + cnt["d"] % 2
                cnt["d"] += 1
                for l in range(nch):
                    S.op("pe", lambda e: e.matmul(C.ps[pd][:], lhsT=bufs["wd"][gp][:, l, dmc * 128:(dmc + 1) * 128],
                                                  rhs=bufs["act"][gp][:, l, sl], start=(l == 0), stop=(l == nch - 1)),
                         reads=[bufs["wd_r"][gp], bufs["act_r"][gp][l][tg]], writes=[C.psr[pd]])
                S.op("dve", lambda e: e.scalar_tensor_tensor(out=hT[:, dmc, sl], in0=C.ps[pd][:], scalar=0.5,
                                                             in1=hT[:, dmc, sl], op0=ALU.mult, op1=ALU.add),
                     reads=[C.psr[pd], hres[dmc][tg]], writes=[hres[dmc][tg]])


def make_ffn_bufs(C):
    b = {"cnt": {"g": 0, "w": 0, "p": 0, "s": 0, "d": 0}}
    b["wd"] = [C.sb("wd%d" % i, [128, 4, 1024], BF16) for i in range(2)]
    b["wd_r"] = [Res() for _ in range(2)]
    b["wgu"] = [C.sb("wgu%d" % i, [128, 2, 8, 128], BF16) for i in range(3)]
    b["wgu_r"] = [Res() for _ in range(3)]
    b["sg"] = [C.sb("sg%d" % i, [128, 512], F32) for i in range(2)]
    b["sg_r"] = [Res() for _ in range(2)]
    b["act"] = [C.sb("actT%d" % i, [128, 4, C.TOK], BF16) for i in range(2)]
    b["act_r"] = [[[Res() for _ in range(C.NTG)] for _ in range(4)] for _ in range(2)]
    return b


def res_grid(n, m):
    return [[Res() for _ in range(m)] for _ in range(n)]


def make_proj_bufs(C, tm_cols):
    b = {"cnt": {"w": 0, "p": 0, "o": 0, "s": 0, "q": 0, "v": 0}}
    b["w"] = [C.sb("pw%d" % i, [128, 8, 128], BF16) for i in range(3)]
    b["w_r"] = [Res() for _ in range(3)]
    b["o"] = [C.sb("po%d" % i, [128, C.TOK], BF16) for i in range(3)]
    b["o_r"] = [Res() for _ in range(3)]
    b["sq"] = [C.sb("psq%d" % i, [128, 512], BF16) for i in range(2)]
    b["sq_r"] = [Res() for _ in range(2)]
    b["rt"] = [C.sb("prt%d" % i, [128, 512], F32) for i in range(2)]
    b["rt_r"] = [Res() for _ in range(2)]
    b["rstd"] = [C.sb("prstd%d" % i, [128, 512], F32) for i in range(2)]
    b["rstd_r"] = [Res() for _ in range(2)]
    b["bones"] = C.sb("bones", [128, 128], BF16)
    b["bones_r"] = Res()
    S = C.S
    S.op("pool", lambda e: e.memset(b["bones"][:], 0.0), writes=[b["bones_r"]])
    S.op("pool", lambda e: e.memset(b["bones"][0:64, 0:64], 1.0), reads=[b["bones_r"]], writes=[b["bones_r"]])
    S.op("pool", lambda e: e.memset(b["bones"][64:128, 64:128], 1.0), reads=[b["bones_r"]], writes=[b["bones_r"]])
    b["biasv"] = C.sb("pbiasv", [128, 4], F32)
    b["biasv_r"] = Res()
    for i, v in enumerate((EPS, 64 * EPS, 128 * EPS)):
        S.op("pool", lambda e: e.memset(b["biasv"][:, i:i + 1], v), reads=[b["biasv_r"]], writes=[b["biasv_r"]])
    if tm_cols:
        b["wtm"] = C.sb("wtm", [128, 8, tm_cols], BF16)
        b["wtm_r"] = Res()
    return b


def fm_chunk(C, xnT, xres, w_ap, M, b, handler):
    S = C.S
    wb = b["cnt"]["w"] % 3
    b["cnt"]["w"] += 1
    S.dma("pool", b["w"][wb][:, :, 0:M], w_ap, writes=[b["w_r"][wb]])
    for tg in range(C.NTG):
        sl = slice(tg * 512, (tg + 1) * 512)
        pb = b["cnt"]["p"] % 2
        b["cnt"]["p"] += 1
        for kc in range(8):
            S.op("pe", lambda e: e.matmul(C.ps[pb][0:M, :], lhsT=b["w"][wb][:, kc, 0:M], rhs=xnT[:, kc, sl],
                                          start=(kc == 0), stop=(kc == 7)),
                 reads=[b["w_r"][wb], xres[kc][tg]], writes=[C.psr[pb]])
        handler(tg, sl, pb)


def headnorm(C, b, pb, M, ones_ap, ones_r, scale, bias_col, gain_ap, gain_r, out_ap, out_r):
    S = C.S
    i = b["cnt"]["s"] % 2
    b["cnt"]["s"] += 1
    ssb = 2 + i
    S.op("act", lambda e: e.activation(out=b["sq"][i][0:M, :], in_=C.ps[pb][0:M, :], func=AF.Square),
         reads=[C.psr[pb]], writes=[b["sq_r"][i]])
    S.op("pe", lambda e: e.matmul(C.ps[ssb][0:M, :], lhsT=ones_ap, rhs=b["sq"][i][0:M, :], start=True, stop=True),
         reads=[b["sq_r"][i], ones_r], writes=[C.psr[ssb]])
    S.op("act", lambda e: e.activation(out=b["rt"][i][0:M, :], in_=C.ps[ssb][0:M, :], func=AF.Sqrt,
                                       scale=scale, bias=b["biasv"][0:M, bias_col:bias_col + 1]),
         reads=[C.psr[ssb], b["biasv_r"]], writes=[b["rt_r"][i]])
    S.op("dve", lambda e: e.reciprocal(out=b["rstd"][i][0:M, :], in_=b["rt"][i][0:M, :]),
         reads=[b["rt_r"][i]], writes=[b["rstd_r"][i]])
    S.op("dve", lambda e: e.scalar_tensor_tensor(out=out_ap, in0=C.ps[pb][0:M, :], scalar=gain_ap,
                                                 in1=b["rstd"][i][0:M, :], op0=ALU.mult, op1=ALU.mult),
         reads=[C.psr[pb], gain_r, b["rstd_r"][i]], writes=[out_r])


SCALE_IQ = (64 ** -0.5) * (8 ** -0.5)


def out_chunk_begin(b):
    i = b["cnt"]["o"] % 3
    b["cnt"]["o"] += 1
    return i


def proj_l0(C, xnT, xres, D, b):
    S = C.S
    TOK = C.TOK
    hg = C.sb("hg", [128, 5], F32); hg_r = Res()
    S.dma("sp", hg[:], D["hg0"], writes=[hg_r])
    sel = C.sb("sel", [8, 4, 128], F32); sel_r = Res()
    S.dma("sp", sel[:], D["sel"], writes=[sel_r])
    wabsT = C.sb("wabsT", [8, TOK], F32); wabs_r = [Res() for _ in range(C.NTG)]
    wbc = [C.sb("wbc%d" % i, [128, 512], F32) for i in range(2)]; wbc_r = [Res(), Res()]
    S.dma("pool", b["wtm"][:], D["w_tm0"], writes=[b["wtm_r"]])

    def h_iw(tg, sl, pb):
        S.op("act", lambda e: e.activation(out=wabsT[0:8, sl], in_=C.ps[pb][0:8, :], func=AF.Abs),
             reads=[C.psr[pb]], writes=[wabs_r[tg]])
    fm_chunk(C, xnT, xres, D["w_iw0"], 8, b, h_iw)

    oi = out_chunk_begin(b)

    def h_ik(tg, sl, pb):
        headnorm(C, b, pb, 64, b["bones"][0:64, 0:64], b["bones_r"], 1.0 / 64, 0, hg[0:64, 4:5], hg_r,
                 b["o"][oi][0:64, sl], b["o_r"][oi])
    fm_chunk(C, xnT, xres, D["w_ik0"], 64, b, h_ik)
    S.dma("sp", D["ikT"], b["o"][oi][0:64, :], reads=[b["o_r"][oi]])

    for ch in range(20):
        kind = ch // 4
        sub = ch % 4
        oi = out_chunk_begin(b)
        if kind in (0, 3):
            gcol = 0 if kind == 0 else 2

            def h(tg, sl, pb):
                headnorm(C, b, pb, 128, b["bones"][:], b["bones_r"], 1.0, 1, hg[:, gcol:gcol + 1], hg_r,
                         b["o"][oi][:, sl], b["o_r"][oi])
        elif kind in (1, 4):
            gcol = 1 if kind == 1 else 3

            def h(tg, sl, pb):
                headnorm(C, b, pb, 128, b["bones"][:], b["bones_r"], 1.0 / 64, 0, hg[:, gcol:gcol + 1], hg_r,
                         b["o"][oi][:, sl], b["o_r"][oi])
        else:
            def h(tg, sl, pb):
                i = b["cnt"]["q"] % 2
                b["cnt"]["q"] += 1
                S.op("pe", lambda e: e.matmul(C.ps[4][:], lhsT=sel[0:8, sub, :], rhs=wabsT[0:8, sl],
                                              start=True, stop=True),
                     reads=[sel_r, wabs_r[tg]], writes=[C.psr[4]])
                S.op("act", lambda e: e.activation(out=wbc[i][:], in_=C.ps[4][:], func=AF.Copy, scale=SCALE_IQ),
                     reads=[C.psr[4]], writes=[wbc_r[i]])
                S.op("dve", lambda e: e.tensor_tensor(out=b["o"][oi][:, sl], in0=C.ps[pb][:], in1=wbc[i][:],
                                                      op=ALU.mult),
                     reads=[C.psr[pb], wbc_r[i]], writes=[b["o_r"][oi]])
        fm_chunk(C, xnT, xres, D["w_fm0"][ch], 128, b, h)
        if kind in (0, 2, 3):
            qi = {0: 0, 2: 4, 3: 8}[kind] + sub
            S.dma("sp", D["qside"][qi], b["o"][oi][:], reads=[b["o_r"][oi]])
        else:
            ki = (0 if kind == 1 else 4) + sub
            S.dma("sp", D["ksT"][ki], b["o"][oi][:], reads=[b["o_r"][oi]])

    va = [C.sb("va%d" % i, [128, 8, 65], BF16) for i in range(2)]; va_r = [Res(), Res()]
    vb = [C.sb("vb%d" % i, [128, 4, 129], BF16) for i in range(2)]; vb_r = [Res(), Res()]
    sg = C.sb("sgn", [128, TOK // 128, 8], F32); sg_r = Res()
    sgt = C.sb("sgt", [128, 8], F32); sgt_r = Res()
    for i in range(2):
        S.op("pool", lambda e: e.memset(va[i][:], 1.0), writes=[va_r[i]])
        S.op("pool", lambda e: e.memset(vb[i][:], 1.0), writes=[vb_r[i]])
    for sl_i in range(TOK // 128):
        ts = slice(sl_i * 128, (sl_i + 1) * 128)
        i = sl_i % 2
        for (bank, c0, n) in ((5, 0, 512), (6, 512, 512), (7, 1024, 8)):
            for kc in range(8):
                S.op("pe", lambda e: e.matmul(C.ps[bank][:, 0:n], lhsT=xnT[:, kc, ts], rhs=b["wtm"][:, kc, c0:c0 + n],
                                              start=(kc == 0), stop=(kc == 7)),
                     reads=[xres[kc][sl_i // 4], b["wtm_r"]], writes=[C.psr[bank]])
        S.op("act", lambda e: e.activation(out=va[i][:, :, 0:64],
                                           in_=C.ps[5][:, :].rearrange("p (h d) -> p h d", d=64), func=AF.Copy),
             reads=[C.psr[5]], writes=[va_r[i]])
        S.op("dve", lambda e: e.tensor_copy(out=vb[i][:, :, 0:128],
                                            in_=C.ps[6][:, :].rearrange("p (h d) -> p h d", d=128)),
             reads=[C.psr[6]], writes=[vb_r[i]])
        S.op("dve", lambda e: e.tensor_scalar(out=sgt[:], in0=C.ps[7][:, 0:8], scalar1=0.0, scalar2=2.0,
                                              op0=ALU.is_ge, op1=ALU.mult),
             reads=[C.psr[7]], writes=[sgt_r])
        S.op("dve", lambda e: e.tensor_scalar(out=sg[:, sl_i, :], in0=sgt[:], scalar1=-1.0, scalar2=None,
                                              op0=ALU.add),
             reads=[sgt_r], writes=[sg_r])
        S.dma("sp", D["vA"][ts, :], va[i][:].rearrange("p h d -> p (h d)"), reads=[va_r[i]])
        S.dma("sp", D["vB"][ts, :], vb[i][:].rearrange("p h d -> p (h d)"), reads=[vb_r[i]])
    S.dma("sp", D["sgn"].rearrange("(s p) h -> p s h", p=128), sg[:], reads=[sg_r])


def dram_in(nc, name, shape, dt):
    return nc.dram_tensor(name, list(shape), dt, kind="ExternalInput").ap()


def dram_out(nc, name, shape, dt):
    return nc.dram_tensor(name, list(shape), dt, kind="ExternalOutput").ap()


def load_hT(C, hT, hres, src):
    S = C.S
    v = src.rearrange("(c p) t -> p c t", p=128)
    for kc in range(8):
        S.dma("sp", hT[:, kc, :], v[:, kc, :], writes=hres[kc])


def store_hT(C, hT, hres, dst):
    S = C.S
    v = dst.rearrange("(c p) t -> p c t", p=128)
    evs = []
    for kc in range(8):
        evs.append(S.dma("sp", v[:, kc, :], hT[:, kc, :], reads=hres[kc]))
    return evs


def stage1_body(C, D):
    S = C.S
    TOK = C.TOK
    hT = C.sb("hT", [128, 8, TOK], F32); hres = res_grid(8, C.NTG)
    xnT = C.sb("xnT", [128, 8, TOK], BF16); xres = res_grid(8, C.NTG)
    gv = C.sb("gv", [128, 16], F32); gv_r = Res()
    load_hT(C, hT, hres, D["xT"])
    S.dma("sp", gv[:], D["gv1"], writes=[gv_r])
    tmp = make_norm_tmp(C)
    with C.scope():
        fb = make_ffn_bufs(C)
        rmsnorm_fm(C, hT, hres, gv[:, 0:8], gv_r, xnT, xres, tmp)
        ffn_fm(C, hT, hres, xnT, xres, D["wgu_l0f1"], D["wd_l0f1"], fb)
    evs = store_hT(C, hT, hres, D["hT1"])
    with C.scope():
        pb = make_proj_bufs(C, 1032)
        rmsnorm_fm(C, hT, hres, gv[:, 8:16], gv_r, xnT, xres, tmp)
        proj_l0(C, xnT, xres, D, pb)
    return evs


def stage1_dram(nc, TOK):
    D = {}
    D["xT"] = dram_in(nc, "xT", [1024, TOK], F32)
    D["gv1"] = dram_in(nc, "gv1", [128, 16], F32)
    D["hg0"] = dram_in(nc, "hg0", [128, 5], F32)
    D["sel"] = dram_in(nc, "sel", [8, 4, 128], F32)
    D["wgu_l0f1"] = dram_in(nc, "wgu_l0f1", [22, 128, 2, 8, 128], F32)
    D["wd_l0f1"] = dram_in(nc, "wd_l0f1", [2816, 1024], F32)
    D["w_fm0"] = dram_in(nc, "w_fm0", [20, 128, 8, 128], F32)
    D["w_ik0"] = dram_in(nc, "w_ik0", [128, 8, 64], F32)
    D["w_iw0"] = dram_in(nc, "w_iw0", [128, 8, 8], F32)
    D["w_tm0"] = dram_in(nc, "w_tm0", [128, 8, 1032], F32)
    D["hT1"] = dram_out(nc, "hT1", [1024, TOK], F32)
    D["qside"] = dram_out(nc, "qside", [12, 128, TOK], BF16)
    D["ksT"] = dram_out(nc, "ksT", [8, 128, TOK], BF16)
    D["ikT"] = dram_out(nc, "ikT", [64, TOK], BF16)
    D["vA"] = dram_out(nc, "vA", [TOK, 520], BF16)
    D["vB"] = dram_out(nc, "vB", [TOK, 516], BF16)
    D["sgn"] = dram_out(nc, "sgn", [TOK, 8], F32)
    return D


def finish_all(C):
    C.S.barrier()


def lay_cols(w):
    n = w.shape[1] // 128
    return np.ascontiguousarray(w.reshape(8, 128, n, 128).transpose(2, 1, 0, 3))


def lay_k(w):
    return np.ascontiguousarray(w.reshape(8, 128, w.shape[1]).transpose(1, 0, 2))


def lay_gain(g):
    return np.ascontiguousarray(g.reshape(8, 128).T)


def lay_wgu(wg, wu):
    return np.ascontiguousarray(np.stack([lay_cols(wg), lay_cols(wu)], axis=2))


def tile2(g):
    return np.concatenate([g, g])


def make_sel():
    sel = np.zeros((8, 4, 128), np.float32)
    for c in range(4):
        for p in range(128):
            sel[2 * c + p // 64, c, p] = 1.0
    return sel


def stage1_inputs(inp, tok_idx):
    f = lambda k: np.asarray(inp[k], np.float32)
    m = {}
    m["xT"] = np.ascontiguousarray(f("x")[0][tok_idx].T)
    m["gv1"] = np.concatenate([lay_gain(f("l0_ffn1_norm")), lay_gain(f("l0_mix_norm"))], axis=1)
    m["hg0"] = np.stack([tile2(f("l0_a_q_norm")), tile2(f("l0_a_k_norm")), tile2(f("l0_b_q_norm")),
                         tile2(f("l0_b_k_norm")), tile2(f("l0_idx_k_norm"))], axis=1)
    m["sel"] = make_sel()
    m["wgu_l0f1"] = lay_wgu(f("l0_ffn1_wg"), f("l0_ffn1_wu"))
    m["wd_l0f1"] = f("l0_ffn1_wd")
    w = f("l0_w_in")
    aq, ak, av, iq, ik, iw, bq, bk, bv = np.split(w, np.cumsum([512, 512, 512, 512, 64, 8, 512, 512])[:], axis=1)
    m["w_fm0"] = lay_cols(np.concatenate([aq, ak, iq, bq, bk], axis=1))
    m["w_ik0"] = lay_k(ik)
    m["w_iw0"] = lay_k(iw)
    m["w_tm0"] = lay_k(np.concatenate([av, bv, iw], axis=1))
    return {k: np.ascontiguousarray(v) for k, v in m.items()}


def proj_resid(C, inT, in_res, nf, w_d, hT, hres, b):
    S = C.S
    for dmc in range(8):
        wb = b["cnt"]["w"] % 3
        b["cnt"]["w"] += 1
        S.dma("pool", b["w"][wb][:, 0:nf, :], w_d[dmc], writes=[b["w_r"][wb]])
        for tg in range(C.NTG):
            sl = slice(tg * 512, (tg + 1) * 512)
            pb = b["cnt"]["p"] % 2
            b["cnt"]["p"] += 1
            for fc in range(nf):
                S.op("pe", lambda e: e.matmul(C.ps[pb][:], lhsT=b["w"][wb][:, fc, :], rhs=inT[:, fc, sl],
                                              start=(fc == 0), stop=(fc == nf - 1)),
                     reads=[b["w_r"][wb], in_res[fc][tg]], writes=[C.psr[pb]])
            S.op("dve", lambda e: e.tensor_tensor(out=hT[:, dmc, sl], in0=C.ps[pb][:], in1=hT[:, dmc, sl], op=ALU.add),
                 reads=[C.psr[pb], hres[dmc][tg]], writes=[hres[dmc][tg]])


def mem_xattn(C, hT, hres, xnT, xres, D, L, b):
    S = C.S
    TOK = C.TOK
    p = "l%d_" % L
    memT = C.sb("memT", [128, 8, 256], F32); memT_r = Res()
    S.dma("sp", memT[:], D["memT"].rearrange("(c p) t -> p c t", p=128), writes=[memT_r])
    msg = C.sb("msg", [128, 8], F32); msg_r = Res()
    S.dma("sp", msg[:], D[p + "msg"], writes=[msg_r])
    mhg = C.sb("mhg", [128, 2], F32); mhg_r = Res()
    S.dma("sp", mhg[:], D[p + "mhg"], writes=[mhg_r])
    wq = C.sb("wq", [128, 8, 512], BF16); wq_r = Res()
    S.dma("pool", wq[:], D[p + "wq"], writes=[wq_r])
    wv = C.sb("wv", [128, 8, 512], BF16); wv_r = Res()
    S.dma("pool", wv[:], D[p + "wv"], writes=[wv_r])
    mnT = C.sb("mnT", [128, 8, 256], BF16); mnT_r = Res()
    kT = C.sb("kTm", [128, 4, 256], BF16); kT_r = Res()
    vm = C.sb("vm", [128, 2, 512], BF16); vm_r = Res()
    for kc in range(8):
        i = kc % 2
        S.op("act", lambda e: e.activation(out=b["sq"][i][:, 0:256], in_=memT[:, kc, :], func=AF.Square),
             reads=[memT_r], writes=[b["sq_r"][i]])
        S.op("pe", lambda e: e.matmul(C.ps[2][:, 0:256], lhsT=C.ones_bf[:], rhs=b["sq"][i][:, 0:256],
                                      start=(kc == 0), stop=(kc == 7)),
             reads=[b["sq_r"][i], C.ones_r], writes=[C.psr[2]])
    S.op("act", lambda e: e.activation(out=b["rt"][0][:, 0:256], in_=C.ps[2][:, 0:256], func=AF.Sqrt,
                                       scale=1.0 / 1024, bias=b["biasv"][:, 0:1]),
         reads=[C.psr[2], b["biasv_r"]], writes=[b["rt_r"][0]])
    S.op("dve", lambda e: e.reciprocal(out=b["rstd"][0][:, 0:256], in_=b["rt"][0][:, 0:256]),
         reads=[b["rt_r"][0]], writes=[b["rstd_r"][0]])
    for kc in range(8):
        S.op("dve", lambda e: e.scalar_tensor_tensor(out=mnT[:, kc, :], in0=memT[:, kc, :], scalar=msg[:, kc:kc + 1],
                                                     in1=b["rstd"][0][:, 0:256], op0=ALU.mult, op1=ALU.mult),
             reads=[memT_r, msg_r, b["rstd_r"][0]], writes=[mnT_r])
    for h in range(4):
        wb = b["cnt"]["w"] % 3
        b["cnt"]["w"] += 1
        S.dma("pool", b["w"][wb][:], D[p + "wk"][h], writes=[b["w_r"][wb]])
        pb = b["cnt"]["p"] % 2
        b["cnt"]["p"] += 1
        for kc in range(8):
            S.op("pe", lambda e: e.matmul(C.ps[pb][:, 0:256], lhsT=b["w"][wb][:, kc, :], rhs=mnT[:, kc, :],
                                          start=(kc == 0), stop=(kc == 7)),
                 reads=[b["w_r"][wb], mnT_r], writes=[C.psr[pb]])
        i = b["cnt"]["s"] % 2
        b["cnt"]["s"] += 1
        S.op("act", lambda e: e.activation(out=b["sq"][i][:, 0:256], in_=C.ps[pb][:, 0:256], func=AF.Square),
             reads=[C.psr[pb]], writes=[b["sq_r"][i]])
        S.op("pe", lambda e: e.matmul(C.ps[2 + i][:, 0:256], lhsT=C.ones_bf[:], rhs=b["sq"][i][:, 0:256],
                                      start=True, stop=True),
             reads=[b["sq_r"][i], C.ones_r], writes=[C.psr[2 + i]])
        S.op("act", lambda e: e.activation(out=b["rt"][i][:, 0:256], in_=C.ps[2 + i][:, 0:256], func=AF.Sqrt,
                                           scale=1.0 / 128, bias=b["biasv"][:, 0:1]),
             reads=[C.psr[2 + i], b["biasv_r"]], writes=[b["rt_r"][i]])
        S.op("dve", lambda e: e.reciprocal(out=b["rstd"][i][:, 0:256], in_=b["rt"][i][:, 0:256]),
             reads=[b["rt_r"][i]], writes=[b["rstd_r"][i]])
        S.op("dve", lambda e: e.scalar_tensor_tensor(out=kT[:, h, :], in0=C.ps[pb][:, 0:256], scalar=mhg[:, 1:2],
                                                     in1=b["rstd"][i][:, 0:256], op0=ALU.mult, op1=ALU.mult),
             reads=[C.psr[pb], mhg_r, b["rstd_r"][i]], writes=[kT_r])
    for mblk in range(2):
        pb = b["cnt"]["p"] % 2
        b["cnt"]["p"] += 1
        for kc in range(8):
            S.op("pe", lambda e: e.matmul(C.ps[pb][:], lhsT=mnT[:, kc, mblk * 128:(mblk + 1) * 128], rhs=wv[:, kc, :],
                                          start=(kc == 0), stop=(kc == 7)),
                 reads=[mnT_r, wv_r], writes=[C.psr[pb]])
        S.op("act", lambda e: e.activation(out=vm[:, mblk, :], in_=C.ps[pb][:], func=AF.Copy),
             reads=[C.psr[pb]], writes=[vm_r])
    moT = C.sb("moT", [128, 4, TOK], BF16); mo_r = res_grid(4, C.NTG)
    qT = [C.sb("mqT%d" % i, [128, 512], BF16) for i in range(2)]; qT_r = [Res(), Res()]
    pT = [C.sb("mpT%d" % i, [128, 512], BF16) for i in range(4)]; pT_r = [Res() for _ in range(4)]
    rden = [C.sb("mrd%d" % i, [128, 512], F32) for i in range(2)]; rden_r = [Res(), Res()]
    n = 0
    for tg in range(C.NTG):
        sl = slice(tg * 512, (tg + 1) * 512)
        for h in range(4):
            pb = b["cnt"]["p"] % 2
            b["cnt"]["p"] += 1
            for kc in range(8):
                S.op("pe", lambda e: e.matmul(C.ps[pb][:], lhsT=wq[:, kc, h * 128:(h + 1) * 128], rhs=xnT[:, kc, sl],
                                              start=(kc == 0), stop=(kc == 7)),
                     reads=[wq_r, xres[kc][tg]], writes=[C.psr[pb]])
            qi = n % 2
            headnorm(C, b, pb, 128, C.ones_bf[:], C.ones_r, 1.0, 2, mhg[:, 0:1], mhg_r, qT[qi][:], qT_r[qi])
            for mblk in range(2):
                sb_ = 4 + mblk
                pi = (2 * n + mblk) % 4
                S.op("pe", lambda e: e.matmul(C.ps[sb_][:], lhsT=kT[:, h, mblk * 128:(mblk + 1) * 128], rhs=qT[qi][:],
                                              start=True, stop=True),
                     reads=[kT_r, qT_r[qi]], writes=[C.psr[sb_]])
                S.op("act", lambda e: e.activation(out=pT[pi][:], in_=C.ps[sb_][:], func=AF.Exp),
                     reads=[C.psr[sb_]], writes=[pT_r[pi]])
            for mblk in range(2):
                pi = (2 * n + mblk) % 4
                S.op("pe", lambda e: e.matmul(C.ps[6][:], lhsT=vm[:, mblk, h * 128:(h + 1) * 128], rhs=pT[pi][:],
                                              start=(mblk == 0), stop=(mblk == 1)),
                     reads=[vm_r, pT_r[pi]], writes=[C.psr[6]])
            for mblk in range(2):
                pi = (2 * n + mblk) % 4
                S.op("pe", lambda e: e.matmul(C.ps[7][:], lhsT=C.ones_bf[:], rhs=pT[pi][:],
                                              start=(mblk == 0), stop=(mblk == 1)),
                     reads=[C.ones_r, pT_r[pi]], writes=[C.psr[7]])
            S.op("dve", lambda e: e.reciprocal(out=rden[qi][:], in_=C.ps[7][:]),
                 reads=[C.psr[7]], writes=[rden_r[qi]])
            S.op("dve", lambda e: e.tensor_tensor(out=moT[:, h, sl], in0=C.ps[6][:], in1=rden[qi][:], op=ALU.mult),
                 reads=[C.psr[6], rden_r[qi]], writes=[mo_r[h][tg]])
            n += 1
    proj_resid(C, moT, mo_r, 4, D[p + "wo"], hT, hres, b)


def mem_inputs(inp, L):
    f = lambda k: np.asarray(inp[k], np.float32)
    p = "l%d_" % L
    m = {}
    m["memT"] = np.ascontiguousarray(f("mem")[0].T)
    m[p + "msg"] = lay_gain(f(p + "mem_src_norm"))
    m[p + "mhg"] = np.stack([f(p + "mem_q_norm"), f(p + "mem_k_norm")], axis=1)
    wkv = f(p + "mem_wkv")
    m[p + "wk"] = lay_cols(wkv[:, :512])
    m[p + "wv"] = lay_k(wkv[:, 512:])
    m[p + "wq"] = lay_k(f(p + "mem_wq"))
    m[p + "wo"] = lay_rows(f(p + "mem_wo"))
    return {k: np.ascontiguousarray(v) for k, v in m.items()}


def lay_rows(w):
    nf = w.shape[0] // 128
    return np.ascontiguousarray(w.reshape(nf, 128, 8, 128).transpose(2, 1, 0, 3))


def mem_dram(nc, D, L, first):
    p = "l%d_" % L
    if first:
        D["memT"] = dram_in(nc, "memT", [1024, 256], F32)
    D[p + "msg"] = dram_in(nc, p + "msg", [128, 8], F32)
    D[p + "mhg"] = dram_in(nc, p + "mhg", [128, 2], F32)
    D[p + "wk"] = dram_in(nc, p + "wk", [4, 128, 8, 128], F32)
    D[p + "wv"] = dram_in(nc, p + "wv", [128, 8, 512], F32)
    D[p + "wq"] = dram_in(nc, p + "wq", [128, 8, 512], F32)
    D[p + "wo"] = dram_in(nc, p + "wo", [8, 128, 4, 128], F32)


def proj_l1(C, xnT, xres, D, b):
    S = C.S
    TOK = C.TOK
    hg = C.sb("hg1", [128, 2], F32); hg_r = Res()
    S.dma("sp", hg[:], D["hg1"], writes=[hg_r])
    S.dma("pool", b["wtm"][:, :, 0:1024], D["w_tm1"], writes=[b["wtm_r"]])
    for ch in range(16):
        kind = ch // 8
        sub = ch % 8
        oi = out_chunk_begin(b)
        if kind == 0:
            def h(tg, sl, pb):
                headnorm(C, b, pb, 128, b["bones"][:], b["bones_r"], 1.0, 1, hg[:, 0:1], hg_r,
                         b["o"][oi][:, sl], b["o_r"][oi])
        else:
            def h(tg, sl, pb):
                headnorm(C, b, pb, 128, b["bones"][:], b["bones_r"], 1.0 / 64, 0, hg[:, 1:2], hg_r,
                         b["o"][oi][:, sl], b["o_r"][oi])
        fm_chunk(C, xnT, xres, D["w_fm1"][ch], 128, b, h)
        S.dma("sp", (D["q1"] if kind == 0 else D["k1"])[sub], b["o"][oi][:], reads=[b["o_r"][oi]])
    va = [C.sb("v1a%d" % i, [128, 16, 65], BF16) for i in range(2)]; va_r = [Res(), Res()]
    for i in range(2):
        S.op("pool", lambda e: e.memset(va[i][:], 1.0), writes=[va_r[i]])
    for sl_i in range(TOK // 128):
        ts = slice(sl_i * 128, (sl_i + 1) * 128)
        i = sl_i % 2
        for half, bank in ((0, 5), (1, 6)):
            for kc in range(8):
                S.op("pe", lambda e: e.matmul(C.ps[bank][:], lhsT=xnT[:, kc, ts],
                                              rhs=b["wtm"][:, kc, half * 512:(half + 1) * 512],
                                              start=(kc == 0), stop=(kc == 7)),
                     reads=[xres[kc][sl_i // 4], b["wtm_r"]], writes=[C.psr[bank]])
        S.op("act", lambda e: e.activation(out=va[i][:, 0:8, 0:64],
                                           in_=C.ps[5][:, :].rearrange("p (h d) -> p h d", d=64), func=AF.Copy),
             reads=[C.psr[5]], writes=[va_r[i]])
        S.op("dve", lambda e: e.tensor_copy(out=va[i][:, 8:16, 0:64],
                                            in_=C.ps[6][:, :].rearrange("p (h d) -> p h d", d=64)),
             reads=[C.psr[6]], writes=[va_r[i]])
        S.dma("sp", D["v1"][ts, :], va[i][:].rearrange("p h d -> p (h d)"), reads=[va_r[i]])


def transpose_to_fm(C, O, O_r, ident, ident_r, OT, OT_res_list, ts, bank):
    S = C.S
    pst = C.ps[bank][:].bitcast(BF16)
    for fc in range(8):
        S.op("pe", lambda e: e.transpose(pst[:, fc * 128:(fc + 1) * 128], O[:, fc * 128:(fc + 1) * 128], ident[:]),
             reads=[O_r, ident_r], writes=[C.psr[bank]])
    S.op("act", lambda e: e.activation(out=OT[:, :, ts], in_=pst.rearrange("p (c t) -> p c t", t=128), func=AF.Copy),
         reads=[C.psr[bank]], writes=OT_res_list)


def band_attention(C, D, OT, OT_r):
    S = C.S
    TOK = C.TOK
    NS = TOK // 128
    ident = C.sb("ident", [128, 128], BF16); ident_r = Res()
    S.dma("pool", ident[:], D["ident"], writes=[ident_r])
    bias = C.sb("bbias", [128, 5, 16, 128], BF16); bias_r = Res()
    for i in range(5):
        S.dma("pool", bias[:, i], D["bandbias"][:, i], writes=[bias_r])
    kt = [C.sb("bk%d" % i, [128, 8, 640], BF16) for i in range(2)]; kt_r = [Res(), Res()]
    vt = [C.sb("bv%d" % i, [128, 5, 1040], BF16) for i in range(2)]; vt_r = [Res(), Res()]
    qt = [C.sb("bq%d" % i, [128, 8, 128], BF16) for i in range(2)]; qt_r = [Res(), Res()]
    pT = [C.sb("bp%d" % i, [128, 512], BF16) for i in range(3)]; pT_r = [Res() for _ in range(3)]
    O = [C.sb("bO%d" % i, [128, 1024], BF16) for i in range(2)]; O_r = [Res(), Res()]
    rden = C.sb("brd", [128, 8], F32); rden_r = Res()
    kv = D["kband"].rearrange("c p t -> p c t")
    qv = D["q1"].rearrange("c p t -> p c t")
    npt = 0
    nst = 0
    for j in range(NS):
        bi = j % 2
        S.dma("sp", kt[bi][:], kv[:, :, 128 * j:128 * j + 640], writes=[kt_r[bi]])
        S.dma("sp", vt[bi][:], D["vband"][128 * j:128 * j + 640, :].rearrange("(i p) f -> p i f", p=128),
              writes=[vt_r[bi]])
        S.dma("sp", qt[bi][:], qv[:, :, 128 * j:128 * j + 128], writes=[qt_r[bi]])
        for ps_ in range(2):
            accb = (4, 5)
            first = [True, True]
            for i in range(5):
                for g in range(2):
                    sbk = nst % 4
                    nst += 1
                    for hh in range(4):
                        h = ps_ * 8 + g * 4 + hh
                        hc, pb = h // 2, 64 * (h % 2)
                        S.op("pe", lambda e: e.matmul(C.ps[sbk][:, hh * 128:(hh + 1) * 128],
                                                      lhsT=kt[bi][pb:pb + 64, hc, i * 128:(i + 1) * 128],
                                                      rhs=qt[bi][pb:pb + 64, hc, :],
                                                      start=(hh == 0), stop=False, skip_group_check=True),
                             reads=[kt_r[bi], qt_r[bi]], writes=[C.psr[sbk]])
                        S.op("pe", lambda e: e.matmul(C.ps[sbk][:, hh * 128:(hh + 1) * 128],
                                                      lhsT=ident[:], rhs=bias[:, i, h, :],
                                                      start=False, stop=True, skip_group_check=True),
                             reads=[ident_r, bias_r], writes=[C.psr[sbk]])
                    pi = npt % 3
                    npt += 1
                    S.op("act", lambda e: e.activation(out=pT[pi][:], in_=C.ps[sbk][:], func=AF.Exp),
                         reads=[C.psr[sbk]], writes=[pT_r[pi]])
                    for hh in range(4):
                        h = ps_ * 8 + g * 4 + hh
                        S.op("pe", lambda e: e.matmul(C.ps[accb[g]][:, hh * 65:(hh + 1) * 65],
                                                      lhsT=pT[pi][:, hh * 128:(hh + 1) * 128],
                                                      rhs=vt[bi][:, i, h * 65:(h + 1) * 65],
                                                      start=first[g], stop=(i == 4), skip_group_check=True),
                             reads=[pT_r[pi], vt_r[bi]], writes=[C.psr[accb[g]]])
                        first[g] = False
            for g in range(2):
                accv = C.ps[accb[g]][:, 0:260].rearrange("p (h d) -> p h d", d=65)
                S.op("dve", lambda e: e.reciprocal(out=rden[:, g * 4:(g + 1) * 4], in_=accv[:, :, 64]),
                     reads=[C.psr[accb[g]]], writes=[rden_r])
                for hh in range(4):
                    h = ps_ * 8 + g * 4 + hh
                    S.op("dve", lambda e: e.tensor_scalar(out=O[bi][:, h * 64:(h + 1) * 64], in0=accv[:, hh, 0:64],
                                                          scalar1=rden[:, g * 4 + hh:g * 4 + hh + 1], scalar2=None,
                                                          op0=ALU.mult),
                         reads=[C.psr[accb[g]], rden_r], writes=[O_r[bi]])
        transpose_to_fm(C, O[bi], O_r[bi], ident, ident_r, OT, [OT_r[fc][j // 4] for fc in range(8)],
                        slice(128 * j, 128 * j + 128), 6 + (j % 2))


def band_bias_host(rel_bias):
    out = np.full((128, 5, 16, 128), -30000.0, np.float32)
    s = np.arange(128)[:, None]
    q = np.arange(128)[None, :]
    for i in range(5):
        rel = 128 * (4 - i) + q - s
        idx = np.clip(rel, -256, 256) + 256
        dchunk = 2 * (i - 4) + s // 64 - q // 64
        valid = (dchunk >= -8) & (dchunk <= 0)
        for h in range(16):
            g = rel_bias[idx, h]
            out[:, i, h, :] = np.where(valid, g, np.float32(-30000.0))
    return out


def stage3_body(C, D):
    S = C.S
    TOK = C.TOK
    hT = C.sb("hT", [128, 8, TOK], F32); hres = res_grid(8, C.NTG)
    gv = C.sb("gv3", [128, 16], F32); gv_r = Res()
    S.dma("sp", gv[:], D["gv3"], writes=[gv_r])
    load_hT(C, hT, hres, D["hT2"])
    tmp = make_norm_tmp(C)
    with C.scope():
        OT = C.sb("OT", [128, 8, TOK], BF16); OT_r = res_grid(8, C.NTG)
        with C.scope():
            band_attention(C, D, OT, OT_r)
        with C.scope():
            pb = make_proj_bufs(C, 0)
            proj_resid(C, OT, OT_r, 8, D["wout1"], hT, hres, pb)
    xnT = C.sb("xnT", [128, 8, TOK], BF16); xres = res_grid(8, C.NTG)
    with C.scope():
        pb = make_proj_bufs(C, 0)
        rmsnorm_fm(C, hT, hres, gv[:, 0:8], gv_r, xnT, xres, tmp)
        mem_xattn(C, hT, hres, xnT, xres, D, 1, pb)
    with C.scope():
        fb = make_ffn_bufs(C)
        rmsnorm_fm(C, hT, hres, gv[:, 8:16], gv_r, xnT, xres, tmp)
        ffn_fm(C, hT, hres, xnT, xres, D["wgu_l1f2"], D["wd_l1f2"], fb)
    store_hT(C, hT, hres, D["outT"])


def stage3_dram(nc, TOK):
    D = {}
    D["hT2"] = dram_in(nc, "hT2", [1024, TOK], F32)
    D["gv3"] = dram_in(nc, "gv3", [128, 16], F32)
    D["q1"] = dram_in(nc, "q1", [8, 128, TOK], BF16)
    D["kband"] = dram_in(nc, "kband", [8, 128, TOK + 512], BF16)
    D["vband"] = dram_in(nc, "vband", [TOK + 512, 1040], BF16)
    D["bandbias"] = dram_in(nc, "bandbias", [128, 5, 16, 128], F32)
    D["ident"] = dram_in(nc, "ident", [128, 128], F32)
    D["wout1"] = dram_in(nc, "wout1", [8, 128, 8, 128], F32)
    mem_dram(nc, D, 1, True)
    D["wgu_l1f2"] = dram_in(nc, "wgu_l1f2", [22, 128, 2, 8, 128], F32)
    D["wd_l1f2"] = dram_in(nc, "wd_l1f2", [2816, 1024], F32)
    D["outT"] = dram_out(nc, "outT", [1024, TOK], F32)
    return D


def stage3_weights(inp):
    f = lambda k: np.asarray(inp[k], np.float32)
    m = {}
    m["gv3"] = np.concatenate([lay_gain(f("l1_mem_norm")), lay_gain(f("l1_ffn2_norm"))], axis=1)
    m["bandbias"] = band_bias_host(f("l1_c_rel_bias"))
    m["ident"] = np.eye(128, dtype=np.float32)
    m["wout1"] = lay_rows(f("l1_w_out"))
    m.update(mem_inputs(inp, 1))
    m["wgu_l1f2"] = lay_wgu(f("l1_ffn2_wg"), f("l1_ffn2_wu"))
    m["wd_l1f2"] = f("l1_ffn2_wd")
    return {k: np.ascontiguousarray(v) for k, v in m.items()}


U8 = mybir.dt.uint8
DBG = {}
AX = mybir.AxisListType
NEG = -30000.0
TOPK = 256
BIS_W = 32.0
BIS_NIT = 21


def att_l0(C, D):
    S = C.S
    TOK = C.TOK
    NS = TOK // 128
    ident = C.sb("ident", [128, 128], BF16); ident_r = Res()
    S.dma("pool", ident[:], D["ident"], writes=[ident_r])
    adm = C.sb("adm", [128, 1024], F32); adm_r = Res()
    S.dma("sp", adm[:], D["adm"], writes=[adm_r])
    sgn = C.sb("sgn", [128, NS, 8], F32); sgn_r = Res()
    S.dma("sp", sgn[:], D["sgn"].rearrange("(s p) h -> p s h", p=128), writes=[sgn_r])
    c15 = C.sb("c15", [128, 12], F32); c15_r = Res()
    S.dma("sp", c15[:], D["c15"], writes=[c15_r])
    nb = C.sb("nbias", [128, 9, 12, 128], BF16); nb_r = Res()
    nbst = [C.sb("nbst%d" % i, [128, 12, 128], F32) for i in range(2)]; nbst_r = [Res(), Res()]
    for i in range(9):
        S.dma("sp", nbst[i % 2][:], D["nbias"][:, i], writes=[nbst_r[i % 2]])
        for h in range(12):
            S.op("dve", lambda e: e.tensor_scalar(out=nb[:, i, h, :], in0=nbst[i % 2][:, h, :], scalar1=c15[:, h:h + 1],
                                                  scalar2=None, op0=ALU.subtract),
                 reads=[nbst_r[i % 2], c15_r], writes=[nb_r])
    lqk = C.sb("lqk", [128, 4, 64], F32); lqk_r = Res()
    S.dma("sp", lqk[:], D["lqk"], writes=[lqk_r])
    bsg = C.sb("bsg", [128, 128], F32); bsg_r = Res()
    S.dma("sp", bsg[:], D["bsg"], writes=[bsg_r])
    lt = C.sb("lt", [128, 8], F32); lt_r = Res()
    ljunk = C.sb("ljunk", [128, 64], F32); lj_r = Res()
    for i in range(2):
        S.op("dve", lambda e: e.scalar_tensor_tensor(out=ljunk[:], in0=lqk[:, 2 * i, :], scalar=1.0, in1=lqk[:, 2 * i + 1, :],
                                                     op0=ALU.mult, op1=ALU.mult, accum_out=lt[:, i:i + 1]),
             reads=[lqk_r], writes=[lj_r, lt_r])
    S.op("act", lambda e: e.activation(out=lt[:, 2:4], in_=lt[:, 0:2], func=AF.Exp), reads=[lt_r], writes=[lt_r])
    S.op("dve", lambda e: e.scalar_tensor_tensor(out=lt[:, 4:5], in0=lt[:, 3:4], scalar=-0.2, in1=lt[:, 2:3],
                                                 op0=ALU.add, op1=ALU.subtract),
         reads=[lt_r], writes=[lt_r])
    S.op("dve", lambda e: e.tensor_scalar(out=bsg[:], in0=bsg[:], scalar1=0.8, scalar2=None, op0=ALU.mult),
         reads=[bsg_r], writes=[bsg_r])
    neglam = lt[:, 4:5]
    score = C.sb("score", [128, 16384], F32); score_r = [Res() for _ in range(32)]
    junk = C.sb("junk", [128, 16384], U8); junk_r = Res()
    ikt = [C.sb("ikt%d" % i, [128, 512], BF16) for i in range(2)]; ikt_r = [Res(), Res()]
    kt = [C.sb("kt%d" % i, [128, 4, 512], BF16) for i in range(2)]; kt_r = [Res(), Res()]
    vt = [C.sb("vt%d" % i, [128, 4, 520], BF16) for i in range(2)]; vt_r = [Res(), Res()]
    qa = [C.sb("qa%d" % i, [128, 4, 128], BF16) for i in range(2)]; qa_r = [Res(), Res()]
    qi_ = [C.sb("qi%d" % i, [128, 4, 128], BF16) for i in range(2)]; qi_r = [Res(), Res()]
    qb = [C.sb("qb%d" % i, [128, 4, 128], BF16) for i in range(2)]; qb_r = [Res(), Res()]
    dg = [C.sb("dg%d" % i, [128, 8, 128], BF16) for i in range(2)]; dg_r = [Res(), Res()]
    Rb = [C.sb("Rb%d" % i, [128, 512], BF16) for i in range(4)]; Rb_r = [Res() for _ in range(4)]
    pT = [C.sb("pT%d" % i, [128, 1024], BF16) for i in range(3)]; pT_r = [Res() for _ in range(3)]
    mk = [C.sb("mk%d" % i, [128, 128], BF16) for i in range(4)]; mk_r = [Res() for _ in range(4)]
    Ot = [C.sb("Ot%d" % i, [128, 1024], BF16) for i in range(2)]; Ot_r = [Res(), Res()]
    OTs = [C.sb("OTs%d" % i, [128, 8, 128], BF16) for i in range(2)]; OTs_r = [Res(), Res()]
    sm = C.sb("bis", [128, 8], F32)
    sm_r = [Res() for _ in range(8)]
    tau = [C.sb("tau%d" % i, [128, 1], F32) for i in range(2)]; tau_r = [Res(), Res()]
    rden = C.sb("rden", [128, 16], F32); rden_r = Res()
    bt = [C.sb("bt%d" % i, [128, 128], F32) for i in range(3)]; bt_r = [Res() for _ in range(3)]
    bs = C.sb("bs", [128, 4], F32); bs_r = Res()
    epsb = C.sb("epsb", [128, 1], F32); epsb_r = Res()
    zt = C.sb("zt", [128, 128], BF16); zt_r = Res()
    S.op("pool", lambda e: e.memset(zt[:], 0.0), writes=[zt_r])
    S.op("pool", lambda e: e.memset(epsb[:], EPS), writes=[epsb_r])
    qv = D["qside"].rearrange("c p t -> p c t")
    akv = D["akT_g"].rearrange("c p t -> p c t")
    bkv = D["bkT_g"].rearrange("c p t -> p c t")
    OTd = D["OT0"].rearrange("c p t -> p c t")
    cn = {"L": 0, "R": 0, "sc": 0, "ik": 0, "kv": 0, "st": 0, "pt": 0, "mk": 0}

    def load_q(j):
        b = j % 2
        ts = slice(128 * j, 128 * j + 128)
        S.dma("sp", qa[b][:], qv[:, 0:4, ts], writes=[qa_r[b]])
        S.dma("sp", qi_[b][:], qv[:, 4:8, ts], writes=[qi_r[b]])
        S.dma("sp", qb[b][:], qv[:, 8:12, ts], writes=[qb_r[b]])
        for h in range(8):
            S.op("dve", lambda e: e.tensor_scalar(out=dg[b][:, h, :], in0=ident[:], scalar1=sgn[:, j, h:h + 1],
                                                  scalar2=None, op0=ALU.mult),
                 reads=[ident_r, sgn_r], writes=[dg_r[b]])

    def indexer(j):
        b = j % 2
        ng = 2 * (j + 1)
        for g in range(ng):
            ib = cn["ik"] % 2
            cn["ik"] += 1
            for half in range(2):
                S.dma("sp", ikt[ib][64 * half:64 * half + 64, :], D["ikT_g"][:, 512 * g:512 * g + 512],
                      writes=[ikt_r[ib]])
            sb_ = 6 + cn["sc"] % 2
            cn["sc"] += 1
            for h in range(8):
                hc, pb = h // 2, 64 * (h % 2)
                lb = cn["L"] % 4
                cn["L"] += 1
                S.op("pe", lambda e: e.matmul(C.ps[lb][:], lhsT=qi_[b][pb:pb + 64, hc, :], rhs=ikt[ib][pb:pb + 64, :],
                                              start=True, stop=True),
                     reads=[qi_r[b], ikt_r[ib]], writes=[C.psr[lb]])
                rb = cn["R"] % 4
                cn["R"] += 1
                S.op("act", lambda e: e.activation(out=Rb[rb][:], in_=C.ps[lb][:], func=AF.Relu),
                     reads=[C.psr[lb]], writes=[Rb_r[rb]])
                S.op("pe", lambda e: e.matmul(C.ps[sb_][:], lhsT=dg[b][:, h, :], rhs=Rb[rb][:],
                                              start=(h == 0), stop=(h == 7)),
                     reads=[dg_r[b], Rb_r[rb]], writes=[C.psr[sb_]])
            gs = slice(512 * g, 512 * g + 512)
            if g >= ng - 2:
                a0 = 512 * (g - (ng - 2))
                S.op("dve", lambda e: e.tensor_tensor(out=score[:, gs], in0=C.ps[sb_][:], in1=adm[:, a0:a0 + 512],
                                                      op=ALU.add),
                     reads=[C.psr[sb_], adm_r], writes=[score_r[g]])
            else:
                S.op("dve", lambda e: e.tensor_copy(out=score[:, gs], in_=C.ps[sb_][:]),
                     reads=[C.psr[sb_]], writes=[score_r[g]])

    def bisect(j):
        N = 1024 * (j + 1)
        ng = 2 * (j + 1)
        sr = score_r[0:ng]
        S.op("dve", lambda e: e.tensor_reduce(out=sm[:, 0:1], in_=score[:, 0:N], axis=AX.X, op=ALU.max),
             reads=sr, writes=[sm_r[0]])
        S.op("dve", lambda e: e.tensor_scalar(out=sm[:, 1:2], in0=sm[:, 0:1], scalar1=-BIS_W, scalar2=None, op0=ALU.add),
             reads=[sm_r[0]], writes=[sm_r[1]])
        for it in range(BIS_NIT):
            w = BIS_W / (2.0 ** (it + 1))
            S.op("dve", lambda e: e.tensor_scalar(out=sm[:, 2:3], in0=sm[:, 1:2], scalar1=w, scalar2=None, op0=ALU.add),
                 reads=[sm_r[1]], writes=[sm_r[2]])
            S.op("dve", lambda e: e.tensor_scalar(out=junk[:, 0:N], in0=score[:, 0:N], scalar1=sm[:, 2:3], scalar2=0.0,
                                                  op0=ALU.is_ge, op1=ALU.add, accum_out=sm[:, 3:4]),
                 reads=sr + [sm_r[2]], writes=[junk_r, sm_r[3]])
            S.op("dve", lambda e: e.tensor_scalar(out=sm[:, 4:5], in0=sm[:, 3:4], scalar1=TOPK - 0.5, scalar2=w,
                                                  op0=ALU.is_ge, op1=ALU.mult),
                 reads=[sm_r[3]], writes=[sm_r[4]])
            last = (it == BIS_NIT - 1)
            dst, dst_r = (tau[j % 2][:, 0:1], tau_r[j % 2]) if last else (sm[:, 1:2], sm_r[1])
            S.op("dve", lambda e: e.tensor_tensor(out=dst, in0=sm[:, 1:2], in1=sm[:, 4:5], op=ALU.add),
                 reads=[sm_r[1], sm_r[4]], writes=[dst_r])

    def sweep(j, which):
        b = j % 2
        ng = 2 * (j + 1)
        qt, qt_r = (qa[b], qa_r[b]) if which == 0 else (qb[b], qb_r[b])
        kview = akv if which == 0 else bkv
        vsrc = D["vA_g"] if which == 0 else D["vB_g"]
        vw = 520 if which == 0 else 516
        dv = 65 if which == 0 else 129
        per_bank = 4 if which == 0 else 3
        accb = (4, 5) if which == 0 else (4, 5, 6)
        first = {}
        for g in range(ng):
            sb_i = cn["kv"] % 2
            cn["kv"] += 1
            S.dma("sp", kt[sb_i][:], kview[:, :, 512 * g:512 * g + 512], writes=[kt_r[sb_i]])
            S.dma("sp", vt[sb_i][:, :, 0:vw], vsrc[512 * g:512 * g + 512, :].rearrange("(i p) f -> p i f", p=128),
                  writes=[vt_r[sb_i]])
            for ib in range(4):
                kb = 4 * g + ib
                near = kb - (8 * j - 1)
                if which == 0:
                    mi = cn["mk"] % 4
                    cn["mk"] += 1
                    S.op("dve", lambda e: e.tensor_scalar(out=mk[mi][:], in0=score[:, 128 * kb:128 * kb + 128],
                                                          scalar1=tau[b][:, 0:1], scalar2=NEG, op0=ALU.is_lt, op1=ALU.mult),
                         reads=[score_r[g], tau_r[b]], writes=[mk_r[mi]])
                pair = 2 * (cn["st"] % 2)
                cn["st"] += 1
                pres = [C.psr[pair], C.psr[pair + 1]]
                for h in range(8):
                    hc, pb = h // 2, 64 * (h % 2)
                    bank = pair + h // 4
                    reg = C.ps[bank][:, (h % 4) * 128:(h % 4) * 128 + 128]
                    last_is_qk = False
                    S.op("pe", lambda e: e.matmul(reg, lhsT=kt[sb_i][pb:pb + 64, hc, ib * 128:(ib + 1) * 128],
                                                  rhs=qt[pb:pb + 64, hc, :], start=(h % 4 == 0), stop=last_is_qk,
                                                  skip_group_check=True),
                         reads=[kt_r[sb_i], qt_r], writes=[C.psr[bank]])
                    if which == 0:
                        S.op("pe", lambda e: e.matmul(reg, lhsT=mk[mi][:], rhs=ident[:], start=False, stop=(near < 0),
                                                      skip_group_check=True),
                             reads=[mk_r[mi], ident_r], writes=[C.psr[bank]])
                    if near < 0 and which == 1:
                        S.op("pe", lambda e: e.matmul(reg, lhsT=ident[:], rhs=zt[:], start=False, stop=True,
                                                      skip_group_check=True),
                             reads=[ident_r, zt_r], writes=[C.psr[bank]])
                    if near >= 0:
                        bh = h if which == 0 else 8 + h // 2
                        S.op("pe", lambda e: e.matmul(reg, lhsT=ident[:], rhs=nb[:, near, bh, :], start=False, stop=True,
                                                      skip_group_check=True),
                             reads=[ident_r, nb_r], writes=[C.psr[bank]])
                pi = cn["pt"] % 3
                cn["pt"] += 1
                S.op("act", lambda e: e.activation(out=pT[pi][:].rearrange("p (b n) -> p b n", b=2),
                                                   in_=C.psall[:, pair:pair + 2, :], func=AF.Exp),
                     reads=pres, writes=[pT_r[pi]])
                for h in range(8):
                    bank = accb[h // per_bank]
                    off = (h % per_bank) * dv
                    vh = h if which == 0 else h // 2
                    st = bank not in first
                    first[bank] = True
                    S.op("pe", lambda e: e.matmul(C.ps[bank][:, off:off + dv], lhsT=pT[pi][:, h * 128:(h + 1) * 128],
                                                  rhs=vt[sb_i][:, ib, vh * dv:(vh + 1) * dv],
                                                  start=st, stop=(g == ng - 1 and ib == 3), skip_group_check=True),
                         reads=[pT_r[pi], vt_r[sb_i]], writes=[C.psr[bank]])
        if which == 0:
            for g2 in range(2):
                accv = C.ps[accb[g2]][:, 0:260].rearrange("p (h d) -> p h d", d=65)
                S.op("dve", lambda e: e.reciprocal(out=rden[:, g2 * 4:(g2 + 1) * 4], in_=accv[:, :, 64]),
                     reads=[C.psr[accb[g2]]], writes=[rden_r])
                for hh in range(4):
                    h = g2 * 4 + hh
                    S.op("dve", lambda e: e.tensor_scalar(out=Ot[b][:, h * 64:(h + 1) * 64], in0=accv[:, hh, 0:64],
                                                          scalar1=rden[:, h:h + 1], scalar2=None, op0=ALU.mult),
                         reads=[C.psr[accb[g2]], rden_r], writes=[Ot_r[b]])
        else:
            for m in range(8):
                bank = accb[m // 3]
                off = (m % 3) * 129
                S.op("dve", lambda e: e.reciprocal(out=rden[:, 8 + m:9 + m], in_=C.ps[bank][:, off + 128:off + 129]),
                     reads=[C.psr[bank]], writes=[rden_r])
            for hb in range(4):
                m0, m1 = 2 * hb, 2 * hb + 1
                b0, o0 = accb[m0 // 3], (m0 % 3) * 129
                b1, o1 = accb[m1 // 3], (m1 % 3) * 129
                S.op("dve", lambda e: e.tensor_scalar(out=bt[0][:], in0=C.ps[b0][:, o0:o0 + 128], scalar1=rden[:, 8 + m0:9 + m0],
                                                      scalar2=None, op0=ALU.mult),
                     reads=[C.psr[b0], rden_r], writes=[bt_r[0]])
                S.op("dve", lambda e: e.tensor_scalar(out=bt[1][:], in0=C.ps[b1][:, o1:o1 + 128], scalar1=rden[:, 8 + m1:9 + m1],
                                                      scalar2=neglam, op0=ALU.mult, op1=ALU.mult),
                     reads=[C.psr[b1], rden_r, lt_r], writes=[bt_r[1]])
                S.op("dve", lambda e: e.tensor_tensor(out=bt[0][:], in0=bt[0][:], in1=bt[1][:], op=ALU.add),
                     reads=[bt_r[0], bt_r[1]], writes=[bt_r[0]])
                S.op("dve", lambda e: e.scalar_tensor_tensor(out=bt[2][:], in0=bt[0][:], scalar=1.0, in1=bt[0][:],
                                                             op0=ALU.mult, op1=ALU.mult, accum_out=bs[:, 0:1]),
                     reads=[bt_r[0]], writes=[bt_r[2], bs_r])
                S.op("act", lambda e: e.activation(out=bs[:, 1:2], in_=bs[:, 0:1], func=AF.Sqrt, scale=1.0 / 128,
                                                   bias=epsb[:, 0:1]),
                     reads=[bs_r, epsb_r], writes=[bs_r])
                S.op("dve", lambda e: e.reciprocal(out=bs[:, 2:3], in_=bs[:, 1:2]), reads=[bs_r], writes=[bs_r])
                S.op("dve", lambda e: e.scalar_tensor_tensor(out=Ot[b][:, 512 + hb * 128:512 + (hb + 1) * 128], in0=bt[0][:],
                                                             scalar=bs[:, 2:3], in1=bsg[:], op0=ALU.mult, op1=ALU.mult),
                     reads=[bt_r[0], bs_r, bsg_r], writes=[Ot_r[b]])

    def finish_slot(j):
        b = j % 2
        pst = C.ps[7][:].bitcast(BF16)
        for fc in range(8):
            S.op("pe", lambda e: e.transpose(pst[:, fc * 128:(fc + 1) * 128], Ot[b][:, fc * 128:(fc + 1) * 128], ident[:]),
                 reads=[Ot_r[b], ident_r], writes=[C.psr[7]])
        S.op("act", lambda e: e.activation(out=OTs[b][:], in_=pst.rearrange("p (c t) -> p c t", t=128), func=AF.Copy),
             reads=[C.psr[7]], writes=[OTs_r[b]])
        S.dma("sp", OTd[:, :, 128 * j:128 * j + 128], OTs[b][:], reads=[OTs_r[b]])

    def bisect_dbg(j):
        if DBG.get("nobisect"):
            S.op("dve", lambda e: e.memset(tau[j % 2][:], 0.5), writes=[tau_r[j % 2]])
        else:
            bisect(j)
    for b_ in range(2):
        S.op("pool", lambda e: e.memset(Ot[b_][:], 0.0), writes=[Ot_r[b_]])
    load_q(0)
    if not DBG.get("noindex"):
        indexer(0)
    bisect_dbg(0)
    for j in range(NS):
        if not DBG.get("noA"):
            sweep(j, 0)
        if j + 1 < NS:
            load_q(j + 1)
            if not DBG.get("noindex"):
                indexer(j + 1)
        if not DBG.get("noB"):
            sweep(j, 1)
        finish_slot(j)
        if j + 1 < NS:
            bisect_dbg(j + 1)


def dram_scratch(nc, name, shape, dt):
    return nc.dram_tensor(name, list(shape), dt, kind="Internal").ap()


def stage2_body(C, D):
    S = C.S
    TOK = C.TOK
    gv = C.sb("gv2", [128, 32], F32); gv_r = Res()
    S.dma("sp", gv[:], D["gv2"], writes=[gv_r])
    with C.scope():
        att_l0(C, D)
    hT = C.sb("hT", [128, 8, TOK], F32); hres = res_grid(8, C.NTG)
    load_hT(C, hT, hres, D["hT1"])
    tmp = make_norm_tmp(C)
    with C.scope():
        OT = C.sb("OT", [128, 8, TOK], BF16); OT_r = res_grid(8, C.NTG)
        ov = D["OT0"].rearrange("c p t -> p c t")
        for fc in range(8):
            S.dma("sp", OT[:, fc, :], ov[:, fc, :], writes=OT_r[fc])
        pb = make_proj_bufs(C, 0)
        proj_resid(C, OT, OT_r, 8, D["wout0"], hT, hres, pb)
    xnT = C.sb("xnT", [128, 8, TOK], BF16); xres = res_grid(8, C.NTG)
    with C.scope():
        pb = make_proj_bufs(C, 0)
        rmsnorm_fm(C, hT, hres, gv[:, 0:8], gv_r, xnT, xres, tmp)
        mem_xattn(C, hT, hres, xnT, xres, D, 0, pb)
    with C.scope():
        fb = make_ffn_bufs(C)
        rmsnorm_fm(C, hT, hres, gv[:, 8:16], gv_r, xnT, xres, tmp)
        ffn_fm(C, hT, hres, xnT, xres, D["wgu_l0f2"], D["wd_l0f2"], fb)
        rmsnorm_fm(C, hT, hres, gv[:, 16:24], gv_r, xnT, xres, tmp)
        ffn_fm(C, hT, hres, xnT, xres, D["wgu_l1f1"], D["wd_l1f1"], fb)
    store_hT(C, hT, hres, D["hT2"])
    with C.scope():
        pb = make_proj_bufs(C, 1024)
        rmsnorm_fm(C, hT, hres, gv[:, 24:32], gv_r, xnT, xres, tmp)
        proj_l1(C, xnT, xres, D, pb)


def stage2_dram(nc, TOK, SEQ=16384):
    D = {}
    D["gv2"] = dram_in(nc, "gv2", [128, 32], F32)
    D["hT1"] = dram_in(nc, "hT1", [1024, TOK], F32)
    D["qside"] = dram_in(nc, "qside", [12, 128, TOK], BF16)
    D["sgn"] = dram_in(nc, "sgn", [TOK, 8], F32)
    D["akT_g"] = dram_in(nc, "akT_g", [4, 128, SEQ], BF16)
    D["bkT_g"] = dram_in(nc, "bkT_g", [4, 128, SEQ], BF16)
    D["ikT_g"] = dram_in(nc, "ikT_g", [64, SEQ], BF16)
    D["vA_g"] = dram_in(nc, "vA_g", [SEQ, 520], BF16)
    D["vB_g"] = dram_in(nc, "vB_g", [SEQ, 516], BF16)
    D["nbias"] = dram_in(nc, "nbias", [128, 9, 12, 128], F32)
    D["adm"] = dram_in(nc, "adm", [128, 1024], F32)
    D["c15"] = dram_in(nc, "c15", [128, 12], F32)
    D["ident"] = dram_in(nc, "ident", [128, 128], F32)
    D["lqk"] = dram_in(nc, "lqk", [128, 4, 64], F32)
    D["bsg"] = dram_in(nc, "bsg", [128, 128], F32)
    D["OT0"] = dram_scratch(nc, "OT0", [8, 128, TOK], BF16)
    D["wout0"] = dram_in(nc, "wout0", [8, 128, 8, 128], F32)
    mem_dram(nc, D, 0, True)
    D["wgu_l0f2"] = dram_in(nc, "wgu_l0f2", [22, 128, 2, 8, 128], F32)
    D["wd_l0f2"] = dram_in(nc, "wd_l0f2", [2816, 1024], F32)
    D["wgu_l1f1"] = dram_in(nc, "wgu_l1f1", [22, 128, 2, 8, 128], F32)
    D["wd_l1f1"] = dram_in(nc, "wd_l1f1", [2816, 1024], F32)
    D["hg1"] = dram_in(nc, "hg1", [128, 2], F32)
    D["w_fm1"] = dram_in(nc, "w_fm1", [16, 128, 8, 128], F32)
    D["w_tm1"] = dram_in(nc, "w_tm1", [128, 8, 1024], F32)
    D["hT2"] = dram_out(nc, "hT2", [1024, TOK], F32)
    D["q1"] = dram_out(nc, "q1", [8, 128, TOK], BF16)
    D["k1"] = dram_out(nc, "k1", [8, 128, TOK], BF16)
    D["v1"] = dram_out(nc, "v1", [TOK, 1040], BF16)
    return D


def t5_bucket_np(rel):
    nb = 16
    max_exact = 8
    offset = (rel < 0).astype(np.int32) * nb
    n = np.abs(rel)
    nf = np.maximum(n, 1).astype(np.float32)
    large = max_exact + (np.log(nf / np.float32(max_exact)) / np.float32(math.log(128 / 8))
                         * np.float32(nb - max_exact)).astype(np.int32)
    large = np.minimum(large, nb - 1)
    return offset + np.where(n < max_exact, n, large)


def near_bias_host(t5_bias, c):
    out = np.full((128, 9, 12, 128), NEG, np.float32)
    s = np.arange(128)[:, None]
    q = np.arange(128)[None, :]
    for i in range(9):
        qpos = 128 * c + q
        spos = 128 * (i - 1) + s
        vis = (spos // 64) <= (qpos // 64)
        bk = t5_bucket_np((qpos - spos).astype(np.int32))
        for h in range(12):
            out[:, i, h, :] = np.where(vis, t5_bias[bk, h], np.float32(NEG))
    return out


def adm_host(c):
    q = np.arange(128)[:, None]
    s = np.arange(1024)[None, :]
    vis = (s // 64) <= ((128 * c + q) // 64)
    return np.where(vis, np.float32(0.0), np.float32(-1e30)).astype(np.float32)


def stage2_weights(inp):
    f = lambda k: np.asarray(inp[k], np.float32)
    m = {}
    m["gv2"] = np.concatenate([lay_gain(f("l0_mem_norm")), lay_gain(f("l0_ffn2_norm")),
                               lay_gain(f("l1_ffn1_norm")), lay_gain(f("l1_mix_norm"))], axis=1)
    m["c15"] = np.broadcast_to(f("t5_bias")[15:16, :], (128, 12))
    m["ident"] = np.eye(128, dtype=np.float32)
    m["lqk"] = np.broadcast_to(np.stack([f("l0_b_lq1"), f("l0_b_lk1"), f("l0_b_lq2"), f("l0_b_lk2")])[None], (128, 4, 64))
    m["bsg"] = np.broadcast_to(f("l0_b_subln")[None, :], (128, 128))
    m["wout0"] = lay_rows(f("l0_w_out"))
    m.update(mem_inputs(inp, 0))
    m["wgu_l0f2"] = lay_wgu(f("l0_ffn2_wg"), f("l0_ffn2_wu"))
    m["wd_l0f2"] = f("l0_ffn2_wd")
    m["wgu_l1f1"] = lay_wgu(f("l1_ffn1_wg"), f("l1_ffn1_wu"))
    m["wd_l1f1"] = f("l1_ffn1_wd")
    m["hg1"] = np.stack([tile2(f("l1_c_q_norm")), tile2(f("l1_c_k_norm"))], axis=1)
    w = f("l1_w_in")
    m["w_fm1"] = lay_cols(w[:, :2048])
    m["w_tm1"] = lay_k(w[:, 2048:])
    return {k: np.ascontiguousarray(v) for k, v in m.items()}


NCORES = 8
SEQ = 16384
TOKC = SEQ // NCORES


def build_stage(which):
    nc = bass.Bass("TRN2", target_bir_lowering=False)
    D = {1: stage1_dram, 2: stage2_dram, 3: stage3_dram}[which](nc, TOKC)
    with ExitStack() as es:
        C = Ctx(nc, es, TOKC)
        {1: stage1_body, 2: stage2_body, 3: stage3_body}[which](C, D)
        finish_all(C)
    return nc


def interleave_fm(parts):
    a = np.stack(parts, axis=0)
    lead = a.shape[1:-1]
    a = a.reshape((NCORES,) + lead + (16, 128))
    a = np.moveaxis(a, 0, -2)
    return np.ascontiguousarray(a.reshape(lead + (SEQ,)))


def interleave_tm(parts):
    a = np.stack(parts, axis=0).reshape(NCORES, 16, 128, -1)
    return np.ascontiguousarray(a.transpose(1, 0, 2, 3).reshape(SEQ, -1))


def kernel(**inputs):
    inp = {k: np.asarray(v) for k, v in inputs.items()}
    cores = list(range(NCORES))
    nc1 = build_stage(1)
    ims = []
    for c in cores:
        tok_idx = (np.arange(16)[:, None] * 1024 + 128 * c + np.arange(128)[None, :]).reshape(-1)
        ims.append(stage1_inputs(inp, tok_idx))
    r1 = run_bass_kernel_spmd(nc1, ims, core_ids=cores).results
    ks = interleave_fm([r1[c]["ksT"] for c in cores])
    g = {"akT_g": np.ascontiguousarray(ks[0:4]), "bkT_g": np.ascontiguousarray(ks[4:8]),
         "ikT_g": interleave_fm([r1[c]["ikT"] for c in cores]),
         "vA_g": interleave_tm([r1[c]["vA"] for c in cores]),
         "vB_g": interleave_tm([r1[c]["vB"] for c in cores])}
    W2 = stage2_weights(inp)
    t5 = np.asarray(inp["t5_bias"], np.float32)
    nc2 = build_stage(2)
    ims = []
    for c in cores:
        m = dict(W2)
        m.update(g)
        m["hT1"] = r1[c]["hT1"]; m["qside"] = r1[c]["qside"]; m["sgn"] = r1[c]["sgn"]
        m["nbias"] = near_bias_host(t5, c)
        m["adm"] = adm_host(c)
        ims.append(m)
    r2 = run_bass_kernel_spmd(nc2, ims, core_ids=cores).results
    hT2 = interleave_fm([r2[c]["hT2"] for c in cores])
    q1 = interleave_fm([r2[c]["q1"] for c in cores])
    k1 = interleave_fm([r2[c]["k1"] for c in cores])
    v1 = interleave_tm([r2[c]["v1"] for c in cores])
    k1p = np.concatenate([np.zeros((8, 128, 512), k1.dtype), k1], axis=2)
    v1p = np.concatenate([np.zeros((512, 1040), v1.dtype), v1], axis=0)
    W3 = stage3_weights(inp)
    nc3 = build_stage(3)
    ims = []
    for c in cores:
        t0 = TOKC * c
        m = dict(W3)
        m["hT2"] = np.ascontiguousarray(hT2[:, t0:t0 + TOKC])
        m["q1"] = np.ascontiguousarray(q1[:, :, t0:t0 + TOKC])
        m["kband"] = np.ascontiguousarray(k1p[:, :, t0:t0 + TOKC + 512])
        m["vband"] = np.ascontiguousarray(v1p[t0:t0 + TOKC + 512])
        ims.append(m)
    r3 = run_bass_kernel_spmd(nc3, ims, core_ids=cores).results
    out = np.concatenate([r3[c]["outT"].T for c in cores], axis=0)
    return np.ascontiguousarray(out.reshape(1, SEQ, 1024).astype(np.float32))
```

```python
import math
import numpy as np
from contextlib import ExitStack, contextmanager
import concourse.bass as bass
import concourse.mybir as mybir
from concourse.bass_utils import run_bass_kernel_spmd

F32 = mybir.dt.float32
BF16 = mybir.dt.bfloat16
AF = mybir.ActivationFunctionType
ALU = mybir.AluOpType


STRICT = True


class Res:
    __slots__ = ("name", "lw", "lw_eng", "rd")

    def __init__(self, name=""):
        self.name = name
        self.lw = None
        self.lw_eng = None
        self.rd = []


class Eng:
    def __init__(self, name, h, sem, is_pe=False):
        self.name = name
        self.h = h
        self.sem = sem
        self.n = 0
        self.seen = {}
        self.is_pe = is_pe
        self.nwaits = 0
        self.ninst = 0


class Sched:
    def __init__(self, nc, es, n_dma_sems=32):
        self.nc = nc
        self.E = {}
        for name, h in (("pe", nc.tensor), ("act", nc.scalar), ("dve", nc.vector),
                        ("pool", nc.gpsimd), ("sp", nc.sync)):
            sem = es.enter_context(nc.semaphore("c_" + name))
            self.E[name] = Eng(name, h, sem, is_pe=(name == "pe"))
        self.dsems = [es.enter_context(nc.semaphore("d%d" % i)) for i in range(n_dma_sems)]
        self.dcnt = [0] * n_dma_sems
        self.dpool = {"pool": list(range(0, 8)), "sp": list(range(8, n_dma_sems)), "act": list(range(8, n_dma_sems))}
        self.dnext = {"pool": 0, "sp": 0, "act": 0}

    def _deps(self, eng, reads, writes):
        deps = []
        pe_same = (eng is not None and eng.is_pe)
        for r in reads:
            if r.lw is not None:
                if r.lw_eng is eng and pe_same:
                    continue
                deps.append(r.lw)
        for w in writes:
            if w.lw is not None and (STRICT or w.lw_eng is not eng) and not (pe_same and w.lw_eng is eng):
                deps.append(w.lw)
            for ev, e in w.rd:
                if (STRICT or e is not eng) and not (pe_same and e is eng):
                    deps.append(ev)
        return deps

    def _wait(self, eng, deps):
        best = {}
        for key, sem, val in deps:
            if eng.seen.get(key, 0) >= val:
                continue
            if key not in best or best[key][1] < val:
                best[key] = (sem, val)
        for key, (sem, val) in best.items():
            eng.h.wait_ge(sem, val)
            eng.seen[key] = val
            eng.nwaits += 1

    def op(self, ename, fn, reads=(), writes=()):
        eng = self.E[ename]
        self._wait(eng, self._deps(eng, reads, writes))
        inst = fn(eng.h)
        eng.n += 1
        eng.ninst += 1
        inst.then_inc(eng.sem, 1)
        ev = (ename, eng.sem, eng.n)
        for r in reads:
            r.rd.append((ev, eng))
        for w in writes:
            w.lw = ev
            w.lw_eng = eng
            w.rd = []
        return ev

    def dma(self, qname, out, in_, reads=(), writes=()):
        eng = self.E[qname]
        self._wait(eng, self._deps(None, reads, writes))
        lst = self.dpool[qname]
        i = lst[self.dnext[qname] % len(lst)]
        self.dnext[qname] += 1
        inst = eng.h.dma_start(out=out, in_=in_)
        self.dcnt[i] += 16
        inst.then_inc(self.dsems[i], 16)
        eng.ninst += 1
        ev = ("d%d" % i, self.dsems[i], self.dcnt[i])
        for r in reads:
            r.rd.append((ev, None))
        for w in writes:
            w.lw = ev
            w.lw_eng = None
            w.rd = []
        return ev

    def wait_event(self, ename, ev):
        self._wait(self.E[ename], [ev])

    def finish(self, ename, resources):
        eng = self.E[ename]
        deps = [r.lw for r in resources if r.lw is not None]
        self._wait(eng, deps)


def barrier(S):
    for e in S.E.values():
        deps = []
        for f in S.E.values():
            if f is not e and f.n > 0:
                deps.append((f.name, f.sem, f.n))
        for i, s in enumerate(S.dsems):
            if S.dcnt[i] > 0:
                deps.append(("d%d" % i, s, S.dcnt[i]))
        S._wait(e, deps)


Sched.barrier = barrier


EPS = 1e-6
FFG = [(0, 4), (4, 8), (8, 12), (12, 16), (16, 19), (19, 22)]


class Ctx:
    def __init__(self, nc, es, TOK):
        self.nc = nc
        self.es = es
        self.TOK = TOK
        self.uid = 0
        self.stack = [es]
        self.NTG = TOK // 512
        self.S = Sched(nc, es)
        self.psall = es.enter_context(nc.psum_tensor("psall", [128, 8, 512], F32))
        self.ps = [self.psall[:, i, :] for i in range(8)]
        self.psr = [Res("ps%d" % i) for i in range(8)]
        self.ones_bf = es.enter_context(nc.sbuf_tensor("ones_bf", [128, 128], BF16))
        self.ones_r = Res("ones")
        self.S.op("pool", lambda e: e.memset(self.ones_bf[:], 1.0), writes=[self.ones_r])

    def sb(self, name, shape, dt):
        self.uid += 1
        return self.stack[-1].enter_context(self.nc.sbuf_tensor("%s_%d" % (name, self.uid), shape, dt))

    @contextmanager
    def scope(self):
        with ExitStack() as st:
            self.stack.append(st)
            try:
                yield
            finally:
                self.S.barrier()
                self.stack.pop()


def rmsnorm_fm(C, hT, hres, g_ap, g_res, xnT, xres, tmp):
    S = C.S
    for tg in range(C.NTG):
        sl = slice(tg * 512, (tg + 1) * 512)
        pb = 6 + (tg % 2)
        for kc in range(8):
            b = kc % 2
            S.op("act", lambda e: e.activation(out=tmp["sq"][b][:], in_=hT[:, kc, sl], func=AF.Square),
                 reads=[hres[kc][tg]], writes=[tmp["sq_r"][b]])
            S.op("pe", lambda e: e.matmul(C.ps[pb][:], lhsT=C.ones_bf[:], rhs=tmp["sq"][b][:],
                                          start=(kc == 0), stop=(kc == 7)),
                 reads=[tmp["sq_r"][b], C.ones_r], writes=[C.psr[pb]])
        S.op("act", lambda e: e.activation(out=tmp["rt"][:], in_=C.ps[pb][:], func=AF.Sqrt,
                                           scale=1.0 / 1024.0, bias=tmp["eps"][:]),
             reads=[C.psr[pb], tmp["eps_r"]], writes=[tmp["rt_r"]])
        S.op("dve", lambda e: e.reciprocal(out=tmp["rstd"][:], in_=tmp["rt"][:]),
             reads=[tmp["rt_r"]], writes=[tmp["rstd_r"]])
        for kc in range(8):
            S.op("dve", lambda e: e.scalar_tensor_tensor(out=xnT[:, kc, sl], in0=hT[:, kc, sl],
                                                         scalar=g_ap[:, kc:kc + 1], in1=tmp["rstd"][:],
                                                         op0=ALU.mult, op1=ALU.mult),
                 reads=[hres[kc][tg], g_res, tmp["rstd_r"]], writes=[xres[kc][tg]])


def make_norm_tmp(C):
    t = {}
    t["sq"] = [C.sb("nsq%d" % i, [128, 512], BF16) for i in range(2)]
    t["sq_r"] = [Res() for _ in range(2)]
    t["rt"] = C.sb("nrt", [128, 512], F32)
    t["rt_r"] = Res()
    t["rstd"] = C.sb("nrstd", [128, 512], F32)
    t["rstd_r"] = Res()
    t["eps"] = C.sb("neps", [128, 1], F32)
    t["eps_r"] = Res()
    C.S.op("pool", lambda e: e.memset(t["eps"][:], EPS), writes=[t["eps_r"]])
    return t


def ffn_fm(C, hT, hres, xnT, xres, wgu_d, wd_d, bufs):
    S = C.S
    NTG = C.NTG
    wd_v = wd_d.rearrange("(c p) d -> p c d", p=128)
    cnt = bufs["cnt"]
    for gi, (c0, c1) in enumerate(FFG):
        gp = cnt["g"] % 2
        cnt["g"] += 1
        nch = c1 - c0
        S.dma("pool", bufs["wd"][gp][:, 0:nch, :], wd_v[:, c0:c1, :], writes=[bufs["wd_r"][gp]])
        for ffc in range(c0, c1):
            l = ffc - c0
            wb = cnt["w"] % 3
            cnt["w"] += 1
            S.dma("pool", bufs["wgu"][wb][:], wgu_d[ffc], writes=[bufs["wgu_r"][wb]])
            for tg in range(NTG):
                sl = slice(tg * 512, (tg + 1) * 512)
                pp = cnt["p"] % 2
                cnt["p"] += 1
                pg, pu = 2 * pp, 2 * pp + 1
                for which, pbank in ((0, pg), (1, pu)):
                    for kc in range(8):
                        S.op("pe", lambda e: e.matmul(C.ps[pbank][:], lhsT=bufs["wgu"][wb][:, which, kc, :],
                                                      rhs=xnT[:, kc, sl], start=(kc == 0), stop=(kc == 7)),
                             reads=[bufs["wgu_r"][wb], xres[kc][tg]], writes=[C.psr[pbank]])
                sb_ = cnt["s"] % 2
                cnt["s"] += 1
                S.op("act", lambda e: e.activation(out=bufs["sg"][sb_][:], in_=C.ps[pg][:], func=AF.Silu),
                     reads=[C.psr[pg]], writes=[bufs["sg_r"][sb_]])
                S.op("dve", lambda e: e.tensor_tensor(out=bufs["act"][gp][:, l, sl], in0=C.ps[pu][:],
                                                      in1=bufs["sg"][sb_][:], op=ALU.mult),
                     reads=[C.psr[pu], bufs["sg_r"][sb_]], writes=[bufs["act_r"][gp][l][tg]])
        for tg in range(NTG):
            sl = slice(tg * 512, (tg + 1) * 512)
            for dmc in range(8):
                pd = 4 + cnt["d"] % 2
                cnt["d"] += 1
                for l in range(nch):
                    S.op("pe", lambda e: e.matmul(C.ps[pd][:], lhsT=bufs["wd"][gp][:, l, dmc * 128:(dmc + 1) * 128],
                                                  rhs=bufs["act"][gp][:, l, sl], start=(l == 0), stop=(l == nch - 1)),
                         reads=[bufs["wd_r"][gp], bufs["act_r"][gp][l][tg]], writes=[C.psr[pd]])
                S.op("dve", lambda e: e.scalar_tensor_tensor(out=hT[:, dmc, sl], in0=C.ps[pd][:], scalar=0.5,
                                                             in1=hT[:, dmc, sl], op0=ALU.mult, op1=ALU.add),
                     reads=[C.psr[pd], hres[dmc][tg]], writes=[hres[dmc][tg]])


def make_ffn_bufs(C):
    b = {"cnt": {"g": 0, "w": 0, "p": 0, "s": 0, "d": 0}}
    b["wd"] = [C.sb("wd%d" % i, [128, 4, 1024], BF16) for i in range(2)]
    b["wd_r"] = [Res() for _ in range(2)]
    b["wgu"] = [C.sb("wgu%d" % i, [128, 2, 8, 128], BF16) for i in range(3)]
    b["wgu_r"] = [Res() for _ in range(3)]
    b["sg"] = [C.sb("sg%d" % i, [128, 512], F32) for i in range(2)]
    b["sg_r"] = [Res() for _ in range(2)]
    b["act"] = [C.sb("actT%d" % i, [128, 4, C.TOK], BF16) for i in range(2)]
    b["act_r"] = [[[Res() for _ in range(C.NTG)] for _ in range(4)] for _ in range(2)]
    return b


def res_grid(n, m):
    return [[Res() for _ in range(m)] for _ in range(n)]


def make_proj_bufs(C, tm_cols):
    b = {"cnt": {"w": 0, "p": 0, "o": 0, "s": 0, "q": 0, "v": 0}}
    b["w"] = [C.sb("pw%d" % i, [128, 8, 128], BF16) for i in range(3)]
    b["w_r"] = [Res() for _ in range(3)]
    b["o"] = [C.sb("po%d" % i, [128, C.TOK], BF16) for i in range(3)]
    b["o_r"] = [Res() for _ in range(3)]
    b["sq"] = [C.sb("psq%d" % i, [128, 512], BF16) for i in range(2)]
    b["sq_r"] = [Res() for _ in range(2)]
    b["rt"] = [C.sb("prt%d" % i, [128, 512], F32) for i in range(2)]
    b["rt_r"] = [Res() for _ in range(2)]
    b["rstd"] = [C.sb("prstd%d" % i, [128, 512], F32) for i in range(2)]
    b["rstd_r"] = [Res() for _ in range(2)]
    b["bones"] = C.sb("bones", [128, 128], BF16)
    b["bones_r"] = Res()
    S = C.S
    S.op("pool", lambda e: e.memset(b["bones"][:], 0.0), writes=[b["bones_r"]])
    S.op("pool", lambda e: e.memset(b["bones"][0:64, 0:64], 1.0), reads=[b["bones_r"]], writes=[b["bones_r"]])
    S.op("pool", lambda e: e.memset(b["bones"][64:128, 64:128], 1.0), reads=[b["bones_r"]], writes=[b["bones_r"]])
    b["biasv"] = C.sb("pbiasv", [128, 4], F32)
    b["biasv_r"] = Res()
    for i, v in enumerate((EPS, 64 * EPS, 128 * EPS)):
        S.op("pool", lambda e: e.memset(b["biasv"][:, i:i + 1], v), reads=[b["biasv_r"]], writes=[b["biasv_r"]])
    if tm_cols:
        b["wtm"] = C.sb("wtm", [128, 8, tm_cols], BF16)
        b["wtm_r"] = Res()
    return b


def fm_chunk(C, xnT, xres, w_ap, M, b, handler):
    S = C.S
    wb = b["cnt"]["w"] % 3
    b["cnt"]["w"] += 1
    S.dma("pool", b["w"][wb][:, :, 0:M], w_ap, writes=[b["w_r"][wb]])
    for tg in range(C.NTG):
        sl = slice(tg * 512, (tg + 1) * 512)
        pb = b["cnt"]["p"] % 2
        b["cnt"]["p"] += 1
        for kc in range(8):
            S.op("pe", lambda e: e.matmul(C.ps[pb][0:M, :], lhsT=b["w"][wb][:, kc, 0:M], rhs=xnT[:, kc, sl],
                                          start=(kc == 0), stop=(kc == 7)),
                 reads=[b["w_r"][wb], xres[kc][tg]], writes=[C.psr[pb]])
        handler(tg, sl, pb)


def headnorm(C, b, pb, M, ones_ap, ones_r, scale, bias_col, gain_ap, gain_r, out_ap, out_r):
    S = C.S
    i = b["cnt"]["s"] % 2
    b["cnt"]["s"] += 1
    ssb = 2 + i
    S.op("act", lambda e: e.activation(out=b["sq"][i][0:M, :], in_=C.ps[pb][0:M, :], func=AF.Square),
         reads=[C.psr[pb]], writes=[b["sq_r"][i]])
    S.op("pe", lambda e: e.matmul(C.ps[ssb][0:M, :], lhsT=ones_ap, rhs=b["sq"][i][0:M, :], start=True, stop=True),
         reads=[b["sq_r"][i], ones_r], writes=[C.psr[ssb]])
    S.op("act", lambda e: e.activation(out=b["rt"][i][0:M, :], in_=C.ps[ssb][0:M, :], func=AF.Sqrt,
                                       scale=scale, bias=b["biasv"][0:M, bias_col:bias_col + 1]),
         reads=[C.psr[ssb], b["biasv_r"]], writes=[b["rt_r"][i]])
    S.op("dve", lambda e: e.reciprocal(out=b["rstd"][i][0:M, :], in_=b["rt"][i][0:M, :]),
         reads=[b["rt_r"][i]], writes=[b["rstd_r"][i]])
    S.op("dve", lambda e: e.scalar_tensor_tensor(out=out_ap, in0=C.ps[pb][0:M, :], scalar=gain_ap,
                                                 in1=b["rstd"][i][0:M, :], op0=ALU.mult, op1=ALU.mult),
         reads=[C.psr[pb], gain_r, b["rstd_r"][i]], writes=[out_r])


SCALE_IQ = (64 ** -0.5) * (8 ** -0.5)


def out_chunk_begin(b):
    i = b["cnt"]["o"] % 3
    b["cnt"]["o"] += 1
    return i


def proj_l0(C, xnT, xres, D, b):
    S = C.S
    TOK = C.TOK
    hg = C.sb("hg", [128, 5], F32); hg_r = Res()
    S.dma("sp", hg[:], D["hg0"], writes=[hg_r])
    sel = C.sb("sel", [8, 4, 128], F32); sel_r = Res()
    S.dma("sp", sel[:], D["sel"], writes=[sel_r])
    wabsT = C.sb("wabsT", [8, TOK], F32); wabs_r = [Res() for _ in range(C.NTG)]
    wbc = [C.sb("wbc%d" % i, [128, 512], F32) for i in range(2)]; wbc_r = [Res(), Res()]
    S.dma("pool", b["wtm"][:], D["w_tm0"], writes=[b["wtm_r"]])

    def h_iw(tg, sl, pb):
        S.op("act", lambda e: e.activation(out=wabsT[0:8, sl], in_=C.ps[pb][0:8, :], func=AF.Abs),
             reads=[C.psr[pb]], writes=[wabs_r[tg]])
    fm_chunk(C, xnT, xres, D["w_iw0"], 8, b, h_iw)

    oi = out_chunk_begin(b)

    def h_ik(tg, sl, pb):
        headnorm(C, b, pb, 64, b["bones"][0:64, 0:64], b["bones_r"], 1.0 / 64, 0, hg[0:64, 4:5], hg_r,
                 b["o"][oi][0:64, sl], b["o_r"][oi])
    fm_chunk(C, xnT, xres, D["w_ik0"], 64, b, h_ik)
    S.dma("sp", D["ikT"], b["o"][oi][0:64, :], reads=[b["o_r"][oi]])

    for ch in range(20):
        kind = ch // 4
        sub = ch % 4
        oi = out_chunk_begin(b)
        if kind in (0, 3):
            gcol = 0 if kind == 0 else 2

            def h(tg, sl, pb):
                headnorm(C, b, pb, 128, b["bones"][:], b["bones_r"], 1.0, 1, hg[:, gcol:gcol + 1], hg_r,
                         b["o"][oi][:, sl], b["o_r"][oi])
        elif kind in (1, 4):
            gcol = 1 if kind == 1 else 3

            def h(tg, sl, pb):
                headnorm(C, b, pb, 128, b["bones"][:], b["bones_r"], 1.0 / 64, 0, hg[:, gcol:gcol + 1], hg_r,
                         b["o"][oi][:, sl], b["o_r"][oi])
        else:
            def h(tg, sl, pb):
                i = b["cnt"]["q"] % 2
                b["cnt"]["q"] += 1
                S.op("pe", lambda e: e.matmul(C.ps[4][:], lhsT=sel[0:8, sub, :], rhs=wabsT[0:8, sl],
                                              start=True, stop=True),
                     reads=[sel_r, wabs_r[tg]], writes=[C.psr[4]])
                S.op("act", lambda e: e.activation(out=wbc[i][:], in_=C.ps[4][:], func=AF.Copy, scale=SCALE_IQ),
                     reads=[C.psr[4]], writes=[wbc_r[i]])
                S.op("dve", lambda e: e.tensor_tensor(out=b["o"][oi][:, sl], in0=C.ps[pb][:], in1=wbc[i][:],
                                                      op=ALU.mult),
                     reads=[C.psr[pb], wbc_r[i]], writes=[b["o_r"][oi]])
        fm_chunk(C, xnT, xres, D["w_fm0"][ch], 128, b, h)
        if kind in (0, 2, 3):
            qi = {0: 0, 2: 4, 3: 8}[kind] + sub
            S.dma("sp", D["qside"][qi], b["o"][oi][:], reads=[b["o_r"][oi]])
        else:
            ki = (0 if kind == 1 else 4) + sub
            S.dma("sp", D["ksT"][ki], b["o"][oi][:], reads=[b["o_r"][oi]])

    va = [C.sb("va%d" % i, [128, 8, 65], BF16) for i in range(2)]; va_r = [Res(), Res()]
    vb = [C.sb("vb%d" % i, [128, 4, 129], BF16) for i in range(2)]; vb_r = [Res(), Res()]
    sg = C.sb("sgn", [128, TOK // 128, 8], F32); sg_r = Res()
    sgt = C.sb("sgt", [128, 8], F32); sgt_r = Res()
    for i in range(2):
        S.op("pool", lambda e: e.memset(va[i][:], 1.0), writes=[va_r[i]])
        S.op("pool", lambda e: e.memset(vb[i][:], 1.0), writes=[vb_r[i]])
    for sl_i in range(TOK // 128):
        ts = slice(sl_i * 128, (sl_i + 1) * 128)
        i = sl_i % 2
        for (bank, c0, n) in ((5, 0, 512), (6, 512, 512), (7, 1024, 8)):
            for kc in range(8):
                S.op("pe", lambda e: e.matmul(C.ps[bank][:, 0:n], lhsT=xnT[:, kc, ts], rhs=b["wtm"][:, kc, c0:c0 + n],
                                              start=(kc == 0), stop=(kc == 7)),
                     reads=[xres[kc][sl_i // 4], b["wtm_r"]], writes=[C.psr[bank]])
        S.op("act", lambda e: e.activation(out=va[i][:, :, 0:64],
                                           in_=C.ps[5][:, :].rearrange("p (h d) -> p h d", d=64), func=AF.Copy),
             reads=[C.psr[5]], writes=[va_r[i]])
        S.op("dve", lambda e: e.tensor_copy(out=vb[i][:, :, 0:128],
                                            in_=C.ps[6][:, :].rearrange("p (h d) -> p h d", d=128)),
             reads=[C.psr[6]], writes=[vb_r[i]])
        S.op("dve", lambda e: e.tensor_scalar(out=sgt[:], in0=C.ps[7][:, 0:8], scalar1=0.0, scalar2=2.0,
                                              op0=ALU.is_ge, op1=ALU.mult),
             reads=[C.psr[7]], writes=[sgt_r])
        S.op("dve", lambda e: e.tensor_scalar(out=sg[:, sl_i, :], in0=sgt[:], scalar1=-1.0, scalar2=None,
                                              op0=ALU.add),
             reads=[sgt_r], writes=[sg_r])
        S.dma("sp", D["vA"][ts, :], va[i][:].rearrange("p h d -> p (h d)"), reads=[va_r[i]])
        S.dma("sp", D["vB"][ts, :], vb[i][:].rearrange("p h d -> p (h d)"), reads=[vb_r[i]])
    S.dma("sp", D["sgn"].rearrange("(s p) h -> p s h", p=128), sg[:], reads=[sg_r])


def dram_in(nc, name, shape, dt):
    return nc.dram_tensor(name, list(shape), dt, kind="ExternalInput").ap()


def dram_out(nc, name, shape, dt):
    return nc.dram_tensor(name, list(shape), dt, kind="ExternalOutput").ap()


def load_hT(C, hT, hres, src):
    S = C.S
    v = src.rearrange("(c p) t -> p c t", p=128)
    for kc in range(8):
        S.dma("sp", hT[:, kc, :], v[:, kc, :], writes=hres[kc])


def store_hT(C, hT, hres, dst):
    S = C.S
    v = dst.rearrange("(c p) t -> p c t", p=128)
    evs = []
    for kc in range(8):
        evs.append(S.dma("sp", v[:, kc, :], hT[:, kc, :], reads=hres[kc]))
    return evs


def stage1_body(C, D):
    S = C.S
    TOK = C.TOK
    hT = C.sb("hT", [128, 8, TOK], F32); hres = res_grid(8, C.NTG)
    xnT = C.sb("xnT", [128, 8, TOK], BF16); xres = res_grid(8, C.NTG)
    gv = C.sb("gv", [128, 16], F32); gv_r = Res()
    load_hT(C, hT, hres, D["xT"])
    S.dma("sp", gv[:], D["gv1"], writes=[gv_r])
    tmp = make_norm_tmp(C)
    with C.scope():
        fb = make_ffn_bufs(C)
        rmsnorm_fm(C, hT, hres, gv[:, 0:8], gv_r, xnT, xres, tmp)
        ffn_fm(C, hT, hres, xnT, xres, D["wgu_l0f1"], D["wd_l0f1"], fb)
    evs = store_hT(C, hT, hres, D["hT1"])
    with C.scope():
        pb = make_proj_bufs(C, 1032)
        rmsnorm_fm(C, hT, hres, gv[:, 8:16], gv_r, xnT, xres, tmp)
        proj_l0(C, xnT, xres, D, pb)
    return evs


def stage1_dram(nc, TOK):
    D = {}
    D["xT"] = dram_in(nc, "xT", [1024, TOK], F32)
    D["gv1"] = dram_in(nc, "gv1", [128, 16], F32)
    D["hg0"] = dram_in(nc, "hg0", [128, 5], F32)
    D["sel"] = dram_in(nc, "sel", [8, 4, 128], F32)
    D["wgu_l0f1"] = dram_in(nc, "wgu_l0f1", [22, 128, 2, 8, 128], F32)
    D["wd_l0f1"] = dram_in(nc, "wd_l0f1", [2816, 1024], F32)
    D["w_fm0"] = dram_in(nc, "w_fm0", [20, 128, 8, 128], F32)
    D["w_ik0"] = dram_in(nc, "w_ik0", [128, 8, 64], F32)
    D["w_iw0"] = dram_in(nc, "w_iw0", [128, 8, 8], F32)
    D["w_tm0"] = dram_in(nc, "w_tm0", [128, 8, 1032], F32)
    D["hT1"] = dram_out(nc, "hT1", [1024, TOK], F32)
    D["qside"] = dram_out(nc, "qside", [12, 128, TOK], BF16)
    D["ksT"] = dram_out(nc, "ksT", [8, 128, TOK], BF16)
    D["ikT"] = dram_out(nc, "ikT", [64, TOK], BF16)
    D["vA"] = dram_out(nc, "vA", [TOK, 520], BF16)
    D["vB"] = dram_out(nc, "vB", [TOK, 516], BF16)
    D["sgn"] = dram_out(nc, "sgn", [TOK, 8], F32)
    return D


def finish_all(C):
    C.S.barrier()


def lay_cols(w):
    n = w.shape[1] // 128
    return np.ascontiguousarray(w.reshape(8, 128, n, 128).transpose(2, 1, 0, 3))


def lay_k(w):
    return np.ascontiguousarray(w.reshape(8, 128, w.shape[1]).transpose(1, 0, 2))


def lay_gain(g):
    return np.ascontiguousarray(g.reshape(8, 128).T)


def lay_wgu(wg, wu):
    return np.ascontiguousarray(np.stack([lay_cols(wg), lay_cols(wu)], axis=2))


def tile2(g):
    return np.concatenate([g, g])


def make_sel():
    sel = np.zeros((8, 4, 128), np.float32)
    for c in range(4):
        for p in range(128):
            sel[2 * c + p // 64, c, p] = 1.0
    return sel


def stage1_inputs(inp, tok_idx):
    f = lambda k: np.asarray(inp[k], np.float32)
    m = {}
    m["xT"] = np.ascontiguousarray(f("x")[0][tok_idx].T)
    m["gv1"] = np.concatenate([lay_gain(f("l0_ffn1_norm")), lay_gain(f("l0_mix_norm"))], axis=1)
    m["hg0"] = np.stack([tile2(f("l0_a_q_norm")), tile2(f("l0_a_k_norm")), tile2(f("l0_b_q_norm")),
                         tile2(f("l0_b_k_norm")), tile2(f("l0_idx_k_norm"))], axis=1)
    m["sel"] = make_sel()
    m["wgu_l0f1"] = lay_wgu(f("l0_ffn1_wg"), f("l0_ffn1_wu"))
    m["wd_l0f1"] = f("l0_ffn1_wd")
    w = f("l0_w_in")
    aq, ak, av, iq, ik, iw, bq, bk, bv = np.split(w, np.cumsum([512, 512, 512, 512, 64, 8, 512, 512])[:], axis=1)
    m["w_fm0"] = lay_cols(np.concatenate([aq, ak, iq, bq, bk], axis=1))
    m["w_ik0"] = lay_k(ik)
    m["w_iw0"] = lay_k(iw)
    m["w_tm0"] = lay_k(np.concatenate([av, bv, iw], axis=1))
    return {k: np.ascontiguousarray(v) for k, v in m.items()}


def proj_resid(C, inT, in_res, nf, w_d, hT, hres, b):
    S = C.S
    for dmc in range(8):
        wb = b["cnt"]["w"] % 3
        b["cnt"]["w"] += 1
        S.dma("pool", b["w"][wb][:, 0:nf, :], w_d[dmc], writes=[b["w_r"][wb]])
        for tg in range(C.NTG):
            sl = slice(tg * 512, (tg + 1) * 512)
            pb = b["cnt"]["p"] % 2
            b["cnt"]["p"] += 1
            for fc in range(nf):
                S.op("pe", lambda e: e.matmul(C.ps[pb][:], lhsT=b["w"][wb][:, fc, :], rhs=inT[:, fc, sl],
                                              start=(fc == 0), stop=(fc == nf - 1)),
                     reads=[b["w_r"][wb], in_res[fc][tg]], writes=[C.psr[pb]])
            S.op("dve", lambda e: e.tensor_tensor(out=hT[:, dmc, sl], in0=C.ps[pb][:], in1=hT[:, dmc, sl], op=ALU.add),
                 reads=[C.psr[pb], hres[dmc][tg]], writes=[hres[dmc][tg]])


def mem_xattn(C, hT, hres, xnT, xres, D, L, b):
    S = C.S
    TOK = C.TOK
    p = "l%d_" % L
    memT = C.sb("memT", [128, 8, 256], F32); memT_r = Res()
    S.dma("sp", memT[:], D["memT"].rearrange("(c p) t -> p c t", p=128), writes=[memT_r])
    msg = C.sb("msg", [128, 8], F32); msg_r = Res()
    S.dma("sp", msg[:], D[p + "msg"], writes=[msg_r])
    mhg = C.sb("mhg", [128, 2], F32); mhg_r = Res()
    S.dma("sp", mhg[:], D[p + "mhg"], writes=[mhg_r])
    wq = C.sb("wq", [128, 8, 512], BF16); wq_r = Res()
    S.dma("pool", wq[:], D[p + "wq"], writes=[wq_r])
    wv = C.sb("wv", [128, 8, 512], BF16); wv_r = Res()
    S.dma("pool", wv[:], D[p + "wv"], writes=[wv_r])
    mnT = C.sb("mnT", [128, 8, 256], BF16); mnT_r = Res()
    kT = C.sb("kTm", [128, 4, 256], BF16); kT_r = Res()
    vm = C.sb("vm", [128, 2, 512], BF16); vm_r = Res()
    for kc in range(8):
        i = kc % 2
        S.op("act", lambda e: e.activation(out=b["sq"][i][:, 0:256], in_=memT[:, kc, :], func=AF.Square),
             reads=[memT_r], writes=[b["sq_r"][i]])
        S.op("pe", lambda e: e.matmul(C.ps[2][:, 0:256], lhsT=C.ones_bf[:], rhs=b["sq"][i][:, 0:256],
                                      start=(kc == 0), stop=(kc == 7)),
             reads=[b["sq_r"][i], C.ones_r], writes=[C.psr[2]])
    S.op("act", lambda e: e.activation(out=b["rt"][0][:, 0:256], in_=C.ps[2][:, 0:256], func=AF.Sqrt,
                                       scale=1.0 / 1024, bias=b["biasv"][:, 0:1]),
         reads=[C.psr[2], b["biasv_r"]], writes=[b["rt_r"][0]])
    S.op("dve", lambda e: e.reciprocal(out=b["rstd"][0][:, 0:256], in_=b["rt"][0][:, 0:256]),
         reads=[b["rt_r"][0]], writes=[b["rstd_r"][0]])
    for kc in range(8):
        S.op("dve", lambda e: e.scalar_tensor_tensor(out=mnT[:, kc, :], in0=memT[:, kc, :], scalar=msg[:, kc:kc + 1],
                                                     in1=b["rstd"][0][:, 0:256], op0=ALU.mult, op1=ALU.mult),
             reads=[memT_r, msg_r, b["rstd_r"][0]], writes=[mnT_r])
    for h in range(4):
        wb = b["cnt"]["w"] % 3
        b["cnt"]["w"] += 1
        S.dma("pool", b["w"][wb][:], D[p + "wk"][h], writes=[b["w_r"][wb]])
        pb = b["cnt"]["p"] % 2
        b["cnt"]["p"] += 1
        for kc in range(8):
            S.op("pe", lambda e: e.matmul(C.ps[pb][:, 0:256], lhsT=b["w"][wb][:, kc, :], rhs=mnT[:, kc, :],
                                          start=(kc == 0), stop=(kc == 7)),
                 reads=[b["w_r"][wb], mnT_r], writes=[C.psr[pb]])
        i = b["cnt"]["s"] % 2
        b["cnt"]["s"] += 1
        S.op("act", lambda e: e.activation(out=b["sq"][i][:, 0:256], in_=C.ps[pb][:, 0:256], func=AF.Square),
             reads=[C.psr[pb]], writes=[b["sq_r"][i]])
        S.op("pe", lambda e: e.matmul(C.ps[2 + i][:, 0:256], lhsT=C.ones_bf[:], rhs=b["sq"][i][:, 0:256],
                                      start=True, stop=True),
             reads=[b["sq_r"][i], C.ones_r], writes=[C.psr[2 + i]])
        S.op("act", lambda e: e.activation(out=b["rt"][i][:, 0:256], in_=C.ps[2 + i][:, 0:256], func=AF.Sqrt,
                                           scale=1.0 / 128, bias=b["biasv"][:, 0:1]),
             reads=[C.psr[2 + i], b["biasv_r"]], writes=[b["rt_r"][i]])
        S.op("dve", lambda e: e.reciprocal(out=b["rstd"][i][:, 0:256], in_=b["rt"][i][:, 0:256]),
             reads=[b["rt_r"][i]], writes=[b["rstd_r"][i]])
        S.op("dve", lambda e: e.scalar_tensor_tensor(out=kT[:, h, :], in0=C.ps[pb][:, 0:256], scalar=mhg[:, 1:2],
                                                     in1=b["rstd"][i][:, 0:256], op0=ALU.mult, op1=ALU.mult),
             reads=[C.psr[pb], mhg_r, b["rstd_r"][i]], writes=[kT_r])
    for mblk in range(2):
        pb = b["cnt"]["p"] % 2
        b["cnt"]["p"] += 1
        for kc in range(8):
            S.op("pe", lambda e: e.matmul(C.ps[pb][:], lhsT=mnT[:, kc, mblk * 128:(mblk + 1) * 128], rhs=wv[:, kc, :],
                                          start=(kc == 0), stop=(kc == 7)),
                 reads=[mnT_r, wv_r], writes=[C.psr[pb]])
        S.op("act", lambda e: e.activation(out=vm[:, mblk, :], in_=C.ps[pb][:], func=AF.Copy),
             reads=[C.psr[pb]], writes=[vm_r])
    moT = C.sb("moT", [128, 4, TOK], BF16); mo_r = res_grid(4, C.NTG)
    qT = [C.sb("mqT%d" % i, [128, 512], BF16) for i in range(2)]; qT_r = [Res(), Res()]
    pT = [C.sb("mpT%d" % i, [128, 512], BF16) for i in range(4)]; pT_r = [Res() for _ in range(4)]
    rden = [C.sb("mrd%d" % i, [128, 512], F32) for i in range(2)]; rden_r = [Res(), Res()]
    n = 0
    for tg in range(C.NTG):
        sl = slice(tg * 512, (tg + 1) * 512)
        for h in range(4):
            pb = b["cnt"]["p"] % 2
            b["cnt"]["p"] += 1
            for kc in range(8):
                S.op("pe", lambda e: e.matmul(C.ps[pb][:], lhsT=wq[:, kc, h * 128:(h + 1) * 128], rhs=xnT[:, kc, sl],
                                              start=(kc == 0), stop=(kc == 7)),
                     reads=[wq_r, xres[kc][tg]], writes=[C.psr[pb]])
            qi = n % 2
            headnorm(C, b, pb, 128, C.ones_bf[:], C.ones_r, 1.0, 2, mhg[:, 0:1], mhg_r, qT[qi][:], qT_r[qi])
            for mblk in range(2):
                sb_ = 4 + mblk
                pi = (2 * n + mblk) % 4
                S.op("pe", lambda e: e.matmul(C.ps[sb_][:], lhsT=kT[:, h, mblk * 128:(mblk + 1) * 128], rhs=qT[qi][:],
                                              start=True, stop=True),
                     reads=[kT_r, qT_r[qi]], writes=[C.psr[sb_]])
                S.op("act", lambda e: e.activation(out=pT[pi][:], in_=C.ps[sb_][:], func=AF.Exp),
                     reads=[C.psr[sb_]], writes=[pT_r[pi]])
            for mblk in range(2):
                pi = (2 * n + mblk) % 4
                S.op("pe", lambda e: e.matmul(C.ps[6][:], lhsT=vm[:, mblk, h * 128:(h + 1) * 128], rhs=pT[pi][:],
                                              start=(mblk == 0), stop=(mblk == 1)),
                     reads=[vm_r, pT_r[pi]], writes=[C.psr[6]])
            for mblk in range(2):
                pi = (2 * n + mblk) % 4
                S.op("pe", lambda e: e.matmul(C.ps[7][:], lhsT=C.ones_bf[:], rhs=pT[pi][:],
                                              start=(mblk == 0), stop=(mblk == 1)),
                     reads=[C.ones_r, pT_r[pi]], writes=[C.psr[7]])
            S.op("dve", lambda e: e.reciprocal(out=rden[qi][:], in_=C.ps[7][:]),
                 reads=[C.psr[7]], writes=[rden_r[qi]])
            S.op("dve", lambda e: e.tensor_tensor(out=moT[:, h, sl], in0=C.ps[6][:], in1=rden[qi][:], op=ALU.mult),
                 reads=[C.psr[6], rden_r[qi]], writes=[mo_r[h][tg]])
            n += 1
    proj_resid(C, moT, mo_r, 4, D[p + "wo"], hT, hres, b)


MEM_NAMES = {0: dict(src="l0_mem_src_norm", qn="l0_mem_q_norm", kn="l0_mem_k_norm", wkv="l0_mem_wkv",
                     wq="l0_mem_wq", wo="l0_mem_wo"),
             1: dict(src="l1_mem_src_norm", qn="l1_mem_q_norm", kn="l1_mem_k_norm", wkv="l1_mem_wkv",
                     wq="l1_mem_wq", wo="l1_mem_wo")}


def mem_inputs(inp, L):
    f = lambda k: np.asarray(inp[k], np.float32)
    p = "l%d_" % L
    nm = MEM_NAMES[L]
    m = {}
    m["memT"] = np.ascontiguousarray(f("mem")[0].T)
    m[p + "msg"] = lay_gain(f(nm["src"]))
    m[p + "mhg"] = np.stack([f(nm["qn"]), f(nm["kn"])], axis=1)
    wkv = f(nm["wkv"])
    m[p + "wk"] = lay_cols(wkv[:, :512])
    m[p + "wv"] = lay_k(wkv[:, 512:])
    m[p + "wq"] = lay_k(f(nm["wq"]))
    m[p + "wo"] = lay_rows(f(nm["wo"]))
    return {k: np.ascontiguousarray(v) for k, v in m.items()}


def lay_rows(w):
    nf = w.shape[0] // 128
    return np.ascontiguousarray(w.reshape(nf, 128, 8, 128).transpose(2, 1, 0, 3))


def mem_dram(nc, D, L, first):
    p = "l%d_" % L
    if first:
        D["memT"] = dram_in(nc, "memT", [1024, 256], F32)
    D[p + "msg"] = dram_in(nc, p + "msg", [128, 8], F32)
    D[p + "mhg"] = dram_in(nc, p + "mhg", [128, 2], F32)
    D[p + "wk"] = dram_in(nc, p + "wk", [4, 128, 8, 128], F32)
    D[p + "wv"] = dram_in(nc, p + "wv", [128, 8, 512], F32)
    D[p + "wq"] = dram_in(nc, p + "wq", [128, 8, 512], F32)
    D[p + "wo"] = dram_in(nc, p + "wo", [8, 128, 4, 128], F32)


def proj_l1(C, xnT, xres, D, b):
    S = C.S
    TOK = C.TOK
    hg = C.sb("hg1", [128, 2], F32); hg_r = Res()
    S.dma("sp", hg[:], D["hg1"], writes=[hg_r])
    S.dma("pool", b["wtm"][:, :, 0:1024], D["w_tm1"], writes=[b["wtm_r"]])
    for ch in range(16):
        kind = ch // 8
        sub = ch % 8
        oi = out_chunk_begin(b)
        if kind == 0:
            def h(tg, sl, pb):
                headnorm(C, b, pb, 128, b["bones"][:], b["bones_r"], 1.0, 1, hg[:, 0:1], hg_r,
                         b["o"][oi][:, sl], b["o_r"][oi])
        else:
            def h(tg, sl, pb):
                headnorm(C, b, pb, 128, b["bones"][:], b["bones_r"], 1.0 / 64, 0, hg[:, 1:2], hg_r,
                         b["o"][oi][:, sl], b["o_r"][oi])
        fm_chunk(C, xnT, xres, D["w_fm1"][ch], 128, b, h)
        S.dma("sp", (D["q1"] if kind == 0 else D["k1"])[sub], b["o"][oi][:], reads=[b["o_r"][oi]])
    va = [C.sb("v1a%d" % i, [128, 16, 65], BF16) for i in range(2)]; va_r = [Res(), Res()]
    for i in range(2):
        S.op("pool", lambda e: e.memset(va[i][:], 1.0), writes=[va_r[i]])
    for sl_i in range(TOK // 128):
        ts = slice(sl_i * 128, (sl_i + 1) * 128)
        i = sl_i % 2
        for half, bank in ((0, 5), (1, 6)):
            for kc in range(8):
                S.op("pe", lambda e: e.matmul(C.ps[bank][:], lhsT=xnT[:, kc, ts],
                                              rhs=b["wtm"][:, kc, half * 512:(half + 1) * 512],
                                              start=(kc == 0), stop=(kc == 7)),
                     reads=[xres[kc][sl_i // 4], b["wtm_r"]], writes=[C.psr[bank]])
        S.op("act", lambda e: e.activation(out=va[i][:, 0:8, 0:64],
                                           in_=C.ps[5][:, :].rearrange("p (h d) -> p h d", d=64), func=AF.Copy),
             reads=[C.psr[5]], writes=[va_r[i]])
        S.op("dve", lambda e: e.tensor_copy(out=va[i][:, 8:16, 0:64],
                                            in_=C.ps[6][:, :].rearrange("p (h d) -> p h d", d=64)),
             reads=[C.psr[6]], writes=[va_r[i]])
        S.dma("sp", D["v1"][ts, :], va[i][:].rearrange("p h d -> p (h d)"), reads=[va_r[i]])


def transpose_to_fm(C, O, O_r, ident, ident_r, OT, OT_res_list, ts, bank):
    S = C.S
    pst = C.ps[bank][:].bitcast(BF16)
    for fc in range(8):
        S.op("pe", lambda e: e.transpose(pst[:, fc * 128:(fc + 1) * 128], O[:, fc * 128:(fc + 1) * 128], ident[:]),
             reads=[O_r, ident_r], writes=[C.psr[bank]])
    S.op("act", lambda e: e.activation(out=OT[:, :, ts], in_=pst.rearrange("p (c t) -> p c t", t=128), func=AF.Copy),
         reads=[C.psr[bank]], writes=OT_res_list)


def band_attention(C, D, OT, OT_r):
    S = C.S
    TOK = C.TOK
    NS = TOK // 128
    ident = C.sb("ident", [128, 128], BF16); ident_r = Res()
    S.dma("pool", ident[:], D["ident"], writes=[ident_r])
    bias = C.sb("bbias", [128, 5, 16, 128], BF16); bias_r = Res()
    for i in range(5):
        S.dma("pool", bias[:, i], D["bandbias"][:, i], writes=[bias_r])
    kt = [C.sb("bk%d" % i, [128, 8, 640], BF16) for i in range(2)]; kt_r = [Res(), Res()]
    vt = [C.sb("bv%d" % i, [128, 5, 1040], BF16) for i in range(2)]; vt_r = [Res(), Res()]
    qt = [C.sb("bq%d" % i, [128, 8, 128], BF16) for i in range(2)]; qt_r = [Res(), Res()]
    pT = [C.sb("bp%d" % i, [128, 512], BF16) for i in range(3)]; pT_r = [Res() for _ in range(3)]
    O = [C.sb("bO%d" % i, [128, 1024], BF16) for i in range(2)]; O_r = [Res(), Res()]
    rden = C.sb("brd", [128, 8], F32); rden_r = Res()
    kv = D["kband"].rearrange("c p t -> p c t")
    qv = D["q1"].rearrange("c p t -> p c t")
    npt = 0
    nst = 0
    for j in range(NS):
        bi = j % 2
        S.dma("sp", kt[bi][:], kv[:, :, 128 * j:128 * j + 640], writes=[kt_r[bi]])
        S.dma("sp", vt[bi][:], D["vband"][128 * j:128 * j + 640, :].rearrange("(i p) f -> p i f", p=128),
              writes=[vt_r[bi]])
        S.dma("sp", qt[bi][:], qv[:, :, 128 * j:128 * j + 128], writes=[qt_r[bi]])
        for ps_ in range(2):
            accb = (4, 5)
            first = [True, True]
            for i in range(5):
                for g in range(2):
                    sbk = nst % 4
                    nst += 1
                    for hh in range(4):
                        h = ps_ * 8 + g * 4 + hh
                        hc, pb = h // 2, 64 * (h % 2)
                        S.op("pe", lambda e: e.matmul(C.ps[sbk][:, hh * 128:(hh + 1) * 128],
                                                      lhsT=kt[bi][pb:pb + 64, hc, i * 128:(i + 1) * 128],
                                                      rhs=qt[bi][pb:pb + 64, hc, :],
                                                      start=(hh == 0), stop=False, skip_group_check=True),
                             reads=[kt_r[bi], qt_r[bi]], writes=[C.psr[sbk]])
                        S.op("pe", lambda e: e.matmul(C.ps[sbk][:, hh * 128:(hh + 1) * 128],
                                                      lhsT=ident[:], rhs=bias[:, i, h, :],
                                                      start=False, stop=True, skip_group_check=True),
                             reads=[ident_r, bias_r], writes=[C.psr[sbk]])
                    pi = npt % 3
                    npt += 1
                    S.op("act", lambda e: e.activation(out=pT[pi][:], in_=C.ps[sbk][:], func=AF.Exp),
                         reads=[C.psr[sbk]], writes=[pT_r[pi]])
                    for hh in range(4):
                        h = ps_ * 8 + g * 4 + hh
                        S.op("pe", lambda e: e.matmul(C.ps[accb[g]][:, hh * 65:(hh + 1) * 65],
                                                      lhsT=pT[pi][:, hh * 128:(hh + 1) * 128],
                                                      rhs=vt[bi][:, i, h * 65:(h + 1) * 65],
                                                      start=first[g], stop=(i == 4), skip_group_check=True),
                             reads=[pT_r[pi], vt_r[bi]], writes=[C.psr[accb[g]]])
                        first[g] = False
            for g in range(2):
                accv = C.ps[accb[g]][:, 0:260].rearrange("p (h d) -> p h d", d=65)
                S.op("dve", lambda e: e.reciprocal(out=rden[:, g * 4:(g + 1) * 4], in_=accv[:, :, 64]),
                     reads=[C.psr[accb[g]]], writes=[rden_r])
                for hh in range(4):
                    h = ps_ * 8 + g * 4 + hh
                    S.op("dve", lambda e: e.tensor_scalar(out=O[bi][:, h * 64:(h + 1) * 64], in0=accv[:, hh, 0:64],
                                                          scalar1=rden[:, g * 4 + hh:g * 4 + hh + 1], scalar2=None,
                                                          op0=ALU.mult),
                         reads=[C.psr[accb[g]], rden_r], writes=[O_r[bi]])
        transpose_to_fm(C, O[bi], O_r[bi], ident, ident_r, OT, [OT_r[fc][j // 4] for fc in range(8)],
                        slice(128 * j, 128 * j + 128), 6 + (j % 2))


def band_bias_host(rel_bias):
    out = np.full((128, 5, 16, 128), -30000.0, np.float32)
    s = np.arange(128)[:, None]
    q = np.arange(128)[None, :]
    for i in range(5):
        rel = 128 * (4 - i) + q - s
        idx = np.clip(rel, -256, 256) + 256
        dchunk = 2 * (i - 4) + s // 64 - q // 64
        valid = (dchunk >= -8) & (dchunk <= 0)
        for h in range(16):
            g = rel_bias[idx, h]
            out[:, i, h, :] = np.where(valid, g, np.float32(-30000.0))
    return out


def stage3_body(C, D):
    S = C.S
    TOK = C.TOK
    hT = C.sb("hT", [128, 8, TOK], F32); hres = res_grid(8, C.NTG)
    gv = C.sb("gv3", [128, 16], F32); gv_r = Res()
    S.dma("sp", gv[:], D["gv3"], writes=[gv_r])
    load_hT(C, hT, hres, D["hT2"])
    tmp = make_norm_tmp(C)
    with C.scope():
        OT = C.sb("OT", [128, 8, TOK], BF16); OT_r = res_grid(8, C.NTG)
        with C.scope():
            band_attention(C, D, OT, OT_r)
        with C.scope():
            pb = make_proj_bufs(C, 0)
            proj_resid(C, OT, OT_r, 8, D["wout1"], hT, hres, pb)
    xnT = C.sb("xnT", [128, 8, TOK], BF16); xres = res_grid(8, C.NTG)
    with C.scope():
        pb = make_proj_bufs(C, 0)
        rmsnorm_fm(C, hT, hres, gv[:, 0:8], gv_r, xnT, xres, tmp)
        mem_xattn(C, hT, hres, xnT, xres, D, 1, pb)
    with C.scope():
        fb = make_ffn_bufs(C)
        rmsnorm_fm(C, hT, hres, gv[:, 8:16], gv_r, xnT, xres, tmp)
        ffn_fm(C, hT, hres, xnT, xres, D["wgu_l1f2"], D["wd_l1f2"], fb)
    store_hT(C, hT, hres, D["outT"])


def stage3_dram(nc, TOK):
    D = {}
    D["hT2"] = dram_in(nc, "hT2", [1024, TOK], F32)
    D["gv3"] = dram_in(nc, "gv3", [128, 16], F32)
    D["q1"] = dram_in(nc, "q1", [8, 128, TOK], BF16)
    D["kband"] = dram_in(nc, "kband", [8, 128, TOK + 512], BF16)
    D["vband"] = dram_in(nc, "vband", [TOK + 512, 1040], BF16)
    D["bandbias"] = dram_in(nc, "bandbias", [128, 5, 16, 128], F32)
    D["ident"] = dram_in(nc, "ident", [128, 128], F32)
    D["wout1"] = dram_in(nc, "wout1", [8, 128, 8, 128], F32)
    mem_dram(nc, D, 1, True)
    D["wgu_l1f2"] = dram_in(nc, "wgu_l1f2", [22, 128, 2, 8, 128], F32)
    D["wd_l1f2"] = dram_in(nc, "wd_l1f2", [2816, 1024], F32)
    D["outT"] = dram_out(nc, "outT", [1024, TOK], F32)
    return D


def stage3_weights(inp):
    f = lambda k: np.asarray(inp[k], np.float32)
    m = {}
    m["gv3"] = np.concatenate([lay_gain(f("l1_mem_norm")), lay_gain(f("l1_ffn2_norm"))], axis=1)
    m["bandbias"] = band_bias_host(f("l1_c_rel_bias"))
    m["ident"] = np.eye(128, dtype=np.float32)
    m["wout1"] = lay_rows(f("l1_w_out"))
    m.update(mem_inputs(inp, 1))
    m["wgu_l1f2"] = lay_wgu(f("l1_ffn2_wg"), f("l1_ffn2_wu"))
    m["wd_l1f2"] = f("l1_ffn2_wd")
    return {k: np.ascontiguousarray(v) for k, v in m.items()}


U8 = mybir.dt.uint8
DBG = {}
AX = mybir.AxisListType
NEG = -30000.0
TOPK = 256
BIS_W = 32.0
BIS_NIT = 21


def att_l0(C, D):
    S = C.S
    TOK = C.TOK
    NS = TOK // 128
    ident = C.sb("ident", [128, 128], BF16); ident_r = Res()
    S.dma("pool", ident[:], D["ident"], writes=[ident_r])
    adm = C.sb("adm", [128, 1024], F32); adm_r = Res()
    S.dma("sp", adm[:], D["adm"], writes=[adm_r])
    sgn = C.sb("sgn", [128, NS, 8], F32); sgn_r = Res()
    S.dma("sp", sgn[:], D["sgn"].rearrange("(s p) h -> p s h", p=128), writes=[sgn_r])
    c15 = C.sb("c15", [128, 12], F32); c15_r = Res()
    S.dma("sp", c15[:], D["c15"], writes=[c15_r])
    nb = C.sb("nbias", [128, 9, 12, 128], BF16); nb_r = Res()
    nbst = [C.sb("nbst%d" % i, [128, 12, 128], F32) for i in range(2)]; nbst_r = [Res(), Res()]
    for i in range(9):
        S.dma("sp", nbst[i % 2][:], D["nbias"][:, i], writes=[nbst_r[i % 2]])
        for h in range(12):
            S.op("dve", lambda e: e.tensor_scalar(out=nb[:, i, h, :], in0=nbst[i % 2][:, h, :], scalar1=c15[:, h:h + 1],
                                                  scalar2=None, op0=ALU.subtract),
                 reads=[nbst_r[i % 2], c15_r], writes=[nb_r])
    lqk = C.sb("lqk", [128, 4, 64], F32); lqk_r = Res()
    S.dma("sp", lqk[:], D["lqk"], writes=[lqk_r])
    bsg = C.sb("bsg", [128, 128], F32); bsg_r = Res()
    S.dma("sp", bsg[:], D["bsg"], writes=[bsg_r])
    lt = C.sb("lt", [128, 8], F32); lt_r = Res()
    ljunk = C.sb("ljunk", [128, 64], F32); lj_r = Res()
    for i in range(2):
        S.op("dve", lambda e: e.scalar_tensor_tensor(out=ljunk[:], in0=lqk[:, 2 * i, :], scalar=1.0, in1=lqk[:, 2 * i + 1, :],
                                                     op0=ALU.mult, op1=ALU.mult, accum_out=lt[:, i:i + 1]),
             reads=[lqk_r], writes=[lj_r, lt_r])
    S.op("act", lambda e: e.activation(out=lt[:, 2:4], in_=lt[:, 0:2], func=AF.Exp), reads=[lt_r], writes=[lt_r])
    S.op("dve", lambda e: e.scalar_tensor_tensor(out=lt[:, 4:5], in0=lt[:, 3:4], scalar=-0.2, in1=lt[:, 2:3],
                                                 op0=ALU.add, op1=ALU.subtract),
         reads=[lt_r], writes=[lt_r])
    S.op("dve", lambda e: e.tensor_scalar(out=bsg[:], in0=bsg[:], scalar1=0.8, scalar2=None, op0=ALU.mult),
         reads=[bsg_r], writes=[bsg_r])
    neglam = lt[:, 4:5]
    score = C.sb("score", [128, 16384], F32); score_r = [Res() for _ in range(32)]
    junk = C.sb("junk", [128, 16384], U8); junk_r = Res()
    ikt = [C.sb("ikt%d" % i, [128, 512], BF16) for i in range(2)]; ikt_r = [Res(), Res()]
    kt = [C.sb("kt%d" % i, [128, 4, 512], BF16) for i in range(2)]; kt_r = [Res(), Res()]
    vt = [C.sb("vt%d" % i, [128, 4, 520], BF16) for i in range(2)]; vt_r = [Res(), Res()]
    qa = [C.sb("qa%d" % i, [128, 4, 128], BF16) for i in range(2)]; qa_r = [Res(), Res()]
    qi_ = [C.sb("qi%d" % i, [128, 4, 128], BF16) for i in range(2)]; qi_r = [Res(), Res()]
    qb = [C.sb("qb%d" % i, [128, 4, 128], BF16) for i in range(2)]; qb_r = [Res(), Res()]
    dg = [C.sb("dg%d" % i, [128, 8, 128], BF16) for i in range(2)]; dg_r = [Res(), Res()]
    Rb = [C.sb("Rb%d" % i, [128, 512], BF16) for i in range(4)]; Rb_r = [Res() for _ in range(4)]
    pT = [C.sb("pT%d" % i, [128, 1024], BF16) for i in range(3)]; pT_r = [Res() for _ in range(3)]
    mk = [C.sb("mk%d" % i, [128, 128], BF16) for i in range(4)]; mk_r = [Res() for _ in range(4)]
    Ot = [C.sb("Ot%d" % i, [128, 1024], BF16) for i in range(2)]; Ot_r = [Res(), Res()]
    OTs = [C.sb("OTs%d" % i, [128, 8, 128], BF16) for i in range(2)]; OTs_r = [Res(), Res()]
    sm = C.sb("bis", [128, 8], F32)
    sm_r = [Res() for _ in range(8)]
    tau = [C.sb("tau%d" % i, [128, 1], F32) for i in range(2)]; tau_r = [Res(), Res()]
    rden = C.sb("rden", [128, 16], F32); rden_r = Res()
    bt = [C.sb("bt%d" % i, [128, 128], F32) for i in range(3)]; bt_r = [Res() for _ in range(3)]
    bs = C.sb("bs", [128, 4], F32); bs_r = Res()
    epsb = C.sb("epsb", [128, 1], F32); epsb_r = Res()
    zt = C.sb("zt", [128, 128], BF16); zt_r = Res()
    S.op("pool", lambda e: e.memset(zt[:], 0.0), writes=[zt_r])
    S.op("pool", lambda e: e.memset(epsb[:], EPS), writes=[epsb_r])
    qv = D["qside"].rearrange("c p t -> p c t")
    akv = D["akT_g"].rearrange("c p t -> p c t")
    bkv = D["bkT_g"].rearrange("c p t -> p c t")
    OTd = D["OT0"].rearrange("c p t -> p c t")
    cn = {"L": 0, "R": 0, "sc": 0, "ik": 0, "kv": 0, "st": 0, "pt": 0, "mk": 0}

    def load_q(j):
        b = j % 2
        ts = slice(128 * j, 128 * j + 128)
        S.dma("sp", qa[b][:], qv[:, 0:4, ts], writes=[qa_r[b]])
        S.dma("sp", qi_[b][:], qv[:, 4:8, ts], writes=[qi_r[b]])
        S.dma("sp", qb[b][:], qv[:, 8:12, ts], writes=[qb_r[b]])
        for h in range(8):
            S.op("dve", lambda e: e.tensor_scalar(out=dg[b][:, h, :], in0=ident[:], scalar1=sgn[:, j, h:h + 1],
                                                  scalar2=None, op0=ALU.mult),
                 reads=[ident_r, sgn_r], writes=[dg_r[b]])

    def indexer(j):
        b = j % 2
        ng = 2 * (j + 1)
        for g in range(ng):
            ib = cn["ik"] % 2
            cn["ik"] += 1
            for half in range(2):
                S.dma("sp", ikt[ib][64 * half:64 * half + 64, :], D["ikT_g"][:, 512 * g:512 * g + 512],
                      writes=[ikt_r[ib]])
            sb_ = 6 + cn["sc"] % 2
            cn["sc"] += 1
            lbs = {}

            def emit_L(h):
                hc, pb = h // 2, 64 * (h % 2)
                lb_ = cn["L"] % 4
                cn["L"] += 1
                lbs[h] = lb_
                S.op("pe", lambda e: e.matmul(C.ps[lb_][:], lhsT=qi_[b][pb:pb + 64, hc, :], rhs=ikt[ib][pb:pb + 64, :],
                                              start=True, stop=True),
                     reads=[qi_r[b], ikt_r[ib]], writes=[C.psr[lb_]])
            for h in range(3):
                emit_L(h)
            for h in range(8):
                if h + 3 < 8:
                    emit_L(h + 3)
                lb = lbs[h]
                rb = cn["R"] % 4
                cn["R"] += 1
                S.op("act", lambda e: e.activation(out=Rb[rb][:], in_=C.ps[lb][:], func=AF.Relu),
                     reads=[C.psr[lb]], writes=[Rb_r[rb]])
                S.op("pe", lambda e: e.matmul(C.ps[sb_][:], lhsT=dg[b][:, h, :], rhs=Rb[rb][:],
                                              start=(h == 0), stop=(h == 7)),
                     reads=[dg_r[b], Rb_r[rb]], writes=[C.psr[sb_]])
            gs = slice(512 * g, 512 * g + 512)
            if g >= ng - 2:
                a0 = 512 * (g - (ng - 2))
                S.op("dve", lambda e: e.tensor_tensor(out=score[:, gs], in0=C.ps[sb_][:], in1=adm[:, a0:a0 + 512],
                                                      op=ALU.add),
                     reads=[C.psr[sb_], adm_r], writes=[score_r[g]])
            else:
                S.op("dve", lambda e: e.tensor_copy(out=score[:, gs], in_=C.ps[sb_][:]),
                     reads=[C.psr[sb_]], writes=[score_r[g]])

    def bisect(j):
        N = 1024 * (j + 1)
        ng = 2 * (j + 1)
        sr = score_r[0:ng]
        S.op("dve", lambda e: e.tensor_reduce(out=sm[:, 0:1], in_=score[:, 0:N], axis=AX.X, op=ALU.max),
             reads=sr, writes=[sm_r[0]])
        S.op("dve", lambda e: e.tensor_scalar(out=sm[:, 1:2], in0=sm[:, 0:1], scalar1=-BIS_W, scalar2=None, op0=ALU.add),
             reads=[sm_r[0]], writes=[sm_r[1]])
        for it in range(BIS_NIT):
            w = BIS_W / (2.0 ** (it + 1))
            S.op("dve", lambda e: e.tensor_scalar(out=sm[:, 2:3], in0=sm[:, 1:2], scalar1=w, scalar2=None, op0=ALU.add),
                 reads=[sm_r[1]], writes=[sm_r[2]])
            S.op("dve", lambda e: e.tensor_scalar(out=junk[:, 0:N], in0=score[:, 0:N], scalar1=sm[:, 2:3], scalar2=0.0,
                                                  op0=ALU.is_ge, op1=ALU.add, accum_out=sm[:, 3:4]),
                 reads=sr + [sm_r[2]], writes=[junk_r, sm_r[3]])
            S.op("dve", lambda e: e.tensor_scalar(out=sm[:, 4:5], in0=sm[:, 3:4], scalar1=TOPK - 0.5, scalar2=w,
                                                  op0=ALU.is_ge, op1=ALU.mult),
                 reads=[sm_r[3]], writes=[sm_r[4]])
            last = (it == BIS_NIT - 1)
            dst, dst_r = (tau[j % 2][:, 0:1], tau_r[j % 2]) if last else (sm[:, 1:2], sm_r[1])
            S.op("dve", lambda e: e.tensor_tensor(out=dst, in0=sm[:, 1:2], in1=sm[:, 4:5], op=ALU.add),
                 reads=[sm_r[1], sm_r[4]], writes=[dst_r])

    def sweep(j, which):
        b = j % 2
        ng = 2 * (j + 1)
        qt, qt_r = (qa[b], qa_r[b]) if which == 0 else (qb[b], qb_r[b])
        kview = akv if which == 0 else bkv
        vsrc = D["vA_g"] if which == 0 else D["vB_g"]
        vw = 520 if which == 0 else 516
        dv = 65 if which == 0 else 129
        per_bank = 4 if which == 0 else 3
        accb = (4, 5) if which == 0 else (4, 5, 6)
        first = {}
        pend = []

        def emit_exp_pv(g, ib, pair, pres, sb_i):
            pi = cn["pt"] % 3
            cn["pt"] += 1
            S.op("act", lambda e: e.activation(out=pT[pi][:].rearrange("p (b n) -> p b n", b=2),
                                               in_=C.psall[:, pair:pair + 2, :], func=AF.Exp),
                 reads=pres, writes=[pT_r[pi]])
            for h in range(8):
                bank = accb[h // per_bank]
                off = (h % per_bank) * dv
                vh = h if which == 0 else h // 2
                st = bank not in first
                first[bank] = True
                S.op("pe", lambda e: e.matmul(C.ps[bank][:, off:off + dv], lhsT=pT[pi][:, h * 128:(h + 1) * 128],
                                              rhs=vt[sb_i][:, ib, vh * dv:(vh + 1) * dv],
                                              start=st, stop=(g == ng - 1 and ib == 3), skip_group_check=True),
                     reads=[pT_r[pi], vt_r[sb_i]], writes=[C.psr[bank]])

        for g in range(ng):
            sb_i = cn["kv"] % 2
            cn["kv"] += 1
            S.dma("sp", kt[sb_i][:], kview[:, :, 512 * g:512 * g + 512], writes=[kt_r[sb_i]])
            S.dma("sp", vt[sb_i][:, :, 0:vw], vsrc[512 * g:512 * g + 512, :].rearrange("(i p) f -> p i f", p=128),
                  writes=[vt_r[sb_i]])
            for ib in range(4):
                if pend:
                    emit_exp_pv(*pend.pop())
                kb = 4 * g + ib
                near = kb - (8 * j - 1)
                if which == 0:
                    mi = cn["mk"] % 4
                    cn["mk"] += 1
                    S.op("dve", lambda e: e.tensor_scalar(out=mk[mi][:], in0=score[:, 128 * kb:128 * kb + 128],
                                                          scalar1=tau[b][:, 0:1], scalar2=NEG, op0=ALU.is_lt, op1=ALU.mult),
                         reads=[score_r[g], tau_r[b]], writes=[mk_r[mi]])
                pair = 2 * (cn["st"] % 2)
                cn["st"] += 1
                pres = [C.psr[pair], C.psr[pair + 1]]
                for h in range(8):
                    hc, pb = h // 2, 64 * (h % 2)
                    bank = pair + h // 4
                    reg = C.ps[bank][:, (h % 4) * 128:(h % 4) * 128 + 128]
                    last_is_qk = False
                    S.op("pe", lambda e: e.matmul(reg, lhsT=kt[sb_i][pb:pb + 64, hc, ib * 128:(ib + 1) * 128],
                                                  rhs=qt[pb:pb + 64, hc, :], start=(h % 4 == 0), stop=last_is_qk,
                                                  skip_group_check=True),
                         reads=[kt_r[sb_i], qt_r], writes=[C.psr[bank]])
                    if which == 0:
                        S.op("pe", lambda e: e.matmul(reg, lhsT=mk[mi][:], rhs=ident[:], start=False, stop=(near < 0),
                                                      skip_group_check=True),
                             reads=[mk_r[mi], ident_r], writes=[C.psr[bank]])
                    if near < 0 and which == 1:
                        S.op("pe", lambda e: e.matmul(reg, lhsT=ident[:], rhs=zt[:], start=False, stop=True,
                                                      skip_group_check=True),
                             reads=[ident_r, zt_r], writes=[C.psr[bank]])
                    if near >= 0:
                        bh = h if which == 0 else 8 + h // 2
                        S.op("pe", lambda e: e.matmul(reg, lhsT=ident[:], rhs=nb[:, near, bh, :], start=False, stop=True,
                                                      skip_group_check=True),
                             reads=[ident_r, nb_r], writes=[C.psr[bank]])
                pend.append((g, ib, pair, pres, sb_i))
        emit_exp_pv(*pend.pop())
        if which == 0:
            for g2 in range(2):
                accv = C.ps[accb[g2]][:, 0:260].rearrange("p (h d) -> p h d", d=65)
                S.op("dve", lambda e: e.reciprocal(out=rden[:, g2 * 4:(g2 + 1) * 4], in_=accv[:, :, 64]),
                     reads=[C.psr[accb[g2]]], writes=[rden_r])
                for hh in range(4):
                    h = g2 * 4 + hh
                    S.op("dve", lambda e: e.tensor_scalar(out=Ot[b][:, h * 64:(h + 1) * 64], in0=accv[:, hh, 0:64],
                                                          scalar1=rden[:, h:h + 1], scalar2=None, op0=ALU.mult),
                         reads=[C.psr[accb[g2]], rden_r], writes=[Ot_r[b]])
        else:
            for m in range(8):
                bank = accb[m // 3]
                off = (m % 3) * 129
                S.op("dve", lambda e: e.reciprocal(out=rden[:, 8 + m:9 + m], in_=C.ps[bank][:, off + 128:off + 129]),
                     reads=[C.psr[bank]], writes=[rden_r])
            for hb in range(4):
                m0, m1 = 2 * hb, 2 * hb + 1
                b0, o0 = accb[m0 // 3], (m0 % 3) * 129
                b1, o1 = accb[m1 // 3], (m1 % 3) * 129
                S.op("dve", lambda e: e.tensor_scalar(out=bt[0][:], in0=C.ps[b0][:, o0:o0 + 128], scalar1=rden[:, 8 + m0:9 + m0],
                                                      scalar2=None, op0=ALU.mult),
                     reads=[C.psr[b0], rden_r], writes=[bt_r[0]])
                S.op("dve", lambda e: e.tensor_scalar(out=bt[1][:], in0=C.ps[b1][:, o1:o1 + 128], scalar1=rden[:, 8 + m1:9 + m1],
                                                      scalar2=neglam, op0=ALU.mult, op1=ALU.mult),
                     reads=[C.psr[b1], rden_r, lt_r], writes=[bt_r[1]])
                S.op("dve", lambda e: e.tensor_tensor(out=bt[0][:], in0=bt[0][:], in1=bt[1][:], op=ALU.add),
                     reads=[bt_r[0], bt_r[1]], writes=[bt_r[0]])
                S.op("dve", lambda e: e.scalar_tensor_tensor(out=bt[2][:], in0=bt[0][:], scalar=1.0, in1=bt[0][:],
                                                             op0=ALU.mult, op1=ALU.mult, accum_out=bs[:, 0:1]),
                     reads=[bt_r[0]], writes=[bt_r[2], bs_r])
                S.op("act", lambda e: e.activation(out=bs[:, 1:2], in_=bs[:, 0:1], func=AF.Sqrt, scale=1.0 / 128,
                                                   bias=epsb[:, 0:1]),
                     reads=[bs_r, epsb_r], writes=[bs_r])
                S.op("dve", lambda e: e.reciprocal(out=bs[:, 2:3], in_=bs[:, 1:2]), reads=[bs_r], writes=[bs_r])
                S.op("dve", lambda e: e.scalar_tensor_tensor(out=Ot[b][:, 512 + hb * 128:512 + (hb + 1) * 128], in0=bt[0][:],
                                                             scalar=bs[:, 2:3], in1=bsg[:], op0=ALU.mult, op1=ALU.mult),
                     reads=[bt_r[0], bs_r, bsg_r], writes=[Ot_r[b]])

    def finish_slot(j):
        b = j % 2
        pst = C.ps[7][:].bitcast(BF16)
        for fc in range(8):
            S.op("pe", lambda e: e.transpose(pst[:, fc * 128:(fc + 1) * 128], Ot[b][:, fc * 128:(fc + 1) * 128], ident[:]),
                 reads=[Ot_r[b], ident_r], writes=[C.psr[7]])
        S.op("act", lambda e: e.activation(out=OTs[b][:], in_=pst.rearrange("p (c t) -> p c t", t=128), func=AF.Copy),
             reads=[C.psr[7]], writes=[OTs_r[b]])
        S.dma("sp", OTd[:, :, 128 * j:128 * j + 128], OTs[b][:], reads=[OTs_r[b]])

    def bisect_dbg(j):
        if DBG.get("nobisect"):
            S.op("dve", lambda e: e.memset(tau[j % 2][:], 0.5), writes=[tau_r[j % 2]])
        else:
            bisect(j)
    for b_ in range(2):
        S.op("pool", lambda e: e.memset(Ot[b_][:], 0.0), writes=[Ot_r[b_]])
    load_q(0)
    if not DBG.get("noindex"):
        indexer(0)
    bisect_dbg(0)
    for j in range(NS):
        if not DBG.get("noA"):
            sweep(j, 0)
        if j + 1 < NS:
            load_q(j + 1)
            if not DBG.get("noindex"):
                indexer(j + 1)
        if j + 1 < NS:
            bisect_dbg(j + 1)
        if not DBG.get("noB"):
            sweep(j, 1)
        finish_slot(j)


def dram_scratch(nc, name, shape, dt):
    return nc.dram_tensor(name, list(shape), dt, kind="Internal").ap()


def stage2_body(C, D):
    S = C.S
    TOK = C.TOK
    gv = C.sb("gv2", [128, 32], F32); gv_r = Res()
    S.dma("sp", gv[:], D["gv2"], writes=[gv_r])
    with C.scope():
        att_l0(C, D)
    hT = C.sb("hT", [128, 8, TOK], F32); hres = res_grid(8, C.NTG)
    load_hT(C, hT, hres, D["hT1"])
    tmp = make_norm_tmp(C)
    with C.scope():
        OT = C.sb("OT", [128, 8, TOK], BF16); OT_r = res_grid(8, C.NTG)
        ov = D["OT0"].rearrange("c p t -> p c t")
        for fc in range(8):
            S.dma("sp", OT[:, fc, :], ov[:, fc, :], writes=OT_r[fc])
        pb = make_proj_bufs(C, 0)
        proj_resid(C, OT, OT_r, 8, D["wout0"], hT, hres, pb)
    xnT = C.sb("xnT", [128, 8, TOK], BF16); xres = res_grid(8, C.NTG)
    with C.scope():
        pb = make_proj_bufs(C, 0)
        rmsnorm_fm(C, hT, hres, gv[:, 0:8], gv_r, xnT, xres, tmp)
        mem_xattn(C, hT, hres, xnT, xres, D, 0, pb)
    with C.scope():
        fb = make_ffn_bufs(C)
        rmsnorm_fm(C, hT, hres, gv[:, 8:16], gv_r, xnT, xres, tmp)
        ffn_fm(C, hT, hres, xnT, xres, D["wgu_l0f2"], D["wd_l0f2"], fb)
        rmsnorm_fm(C, hT, hres, gv[:, 16:24], gv_r, xnT, xres, tmp)
        ffn_fm(C, hT, hres, xnT, xres, D["wgu_l1f1"], D["wd_l1f1"], fb)
    store_hT(C, hT, hres, D["hT2"])
    with C.scope():
        pb = make_proj_bufs(C, 1024)
        rmsnorm_fm(C, hT, hres, gv[:, 24:32], gv_r, xnT, xres, tmp)
        proj_l1(C, xnT, xres, D, pb)


def stage2_dram(nc, TOK, SEQ=16384):
    D = {}
    D["gv2"] = dram_in(nc, "gv2", [128, 32], F32)
    D["hT1"] = dram_in(nc, "hT1", [1024, TOK], F32)
    D["qside"] = dram_in(nc, "qside", [12, 128, TOK], BF16)
    D["sgn"] = dram_in(nc, "sgn", [TOK, 8], F32)
    D["akT_g"] = dram_in(nc, "akT_g", [4, 128, SEQ], BF16)
    D["bkT_g"] = dram_in(nc, "bkT_g", [4, 128, SEQ], BF16)
    D["ikT_g"] = dram_in(nc, "ikT_g", [64, SEQ], BF16)
    D["vA_g"] = dram_in(nc, "vA_g", [SEQ, 520], BF16)
    D["vB_g"] = dram_in(nc, "vB_g", [SEQ, 516], BF16)
    D["nbias"] = dram_in(nc, "nbias", [128, 9, 12, 128], F32)
    D["adm"] = dram_in(nc, "adm", [128, 1024], F32)
    D["c15"] = dram_in(nc, "c15", [128, 12], F32)
    D["ident"] = dram_in(nc, "ident", [128, 128], F32)
    D["lqk"] = dram_in(nc, "lqk", [128, 4, 64], F32)
    D["bsg"] = dram_in(nc, "bsg", [128, 128], F32)
    D["OT0"] = dram_scratch(nc, "OT0", [8, 128, TOK], BF16)
    D["wout0"] = dram_in(nc, "wout0", [8, 128, 8, 128], F32)
    mem_dram(nc, D, 0, True)
    D["wgu_l0f2"] = dram_in(nc, "wgu_l0f2", [22, 128, 2, 8, 128], F32)
    D["wd_l0f2"] = dram_in(nc, "wd_l0f2", [2816, 1024], F32)
    D["wgu_l1f1"] = dram_in(nc, "wgu_l1f1", [22, 128, 2, 8, 128], F32)
    D["wd_l1f1"] = dram_in(nc, "wd_l1f1", [2816, 1024], F32)
    D["hg1"] = dram_in(nc, "hg1", [128, 2], F32)
    D["w_fm1"] = dram_in(nc, "w_fm1", [16, 128, 8, 128], F32)
    D["w_tm1"] = dram_in(nc, "w_tm1", [128, 8, 1024], F32)
    D["hT2"] = dram_out(nc, "hT2", [1024, TOK], F32)
    D["q1"] = dram_out(nc, "q1", [8, 128, TOK], BF16)
    D["k1"] = dram_out(nc, "k1", [8, 128, TOK], BF16)
    D["v1"] = dram_out(nc, "v1", [TOK, 1040], BF16)
    return D


def t5_bucket_np(rel):
    nb = 16
    max_exact = 8
    offset = (rel < 0).astype(np.int32) * nb
    n = np.abs(rel)
    nf = np.maximum(n, 1).astype(np.float32)
    large = max_exact + (np.log(nf / np.float32(max_exact)) / np.float32(math.log(128 / 8))
                         * np.float32(nb - max_exact)).astype(np.int32)
    large = np.minimum(large, nb - 1)
    return offset + np.where(n < max_exact, n, large)


def near_bias_host(t5_bias, c):
    out = np.full((128, 9, 12, 128), NEG, np.float32)
    s = np.arange(128)[:, None]
    q = np.arange(128)[None, :]
    for i in range(9):
        qpos = 128 * c + q
        spos = 128 * (i - 1) + s
        vis = (spos // 64) <= (qpos // 64)
        bk = t5_bucket_np((qpos - spos).astype(np.int32))
        for h in range(12):
            out[:, i, h, :] = np.where(vis, t5_bias[bk, h], np.float32(NEG))
    return out


def adm_host(c):
    q = np.arange(128)[:, None]
    s = np.arange(1024)[None, :]
    vis = (s // 64) <= ((128 * c + q) // 64)
    return np.where(vis, np.float32(0.0), np.float32(-1e30)).astype(np.float32)


def stage2_weights(inp):
    f = lambda k: np.asarray(inp[k], np.float32)
    m = {}
    m["gv2"] = np.concatenate([lay_gain(f("l0_mem_norm")), lay_gain(f("l0_ffn2_norm")),
                               lay_gain(f("l1_ffn1_norm")), lay_gain(f("l1_mix_norm"))], axis=1)
    m["c15"] = np.broadcast_to(f("t5_bias")[15:16, :], (128, 12))
    m["ident"] = np.eye(128, dtype=np.float32)
    m["lqk"] = np.broadcast_to(np.stack([f("l0_b_lq1"), f("l0_b_lk1"), f("l0_b_lq2"), f("l0_b_lk2")])[None], (128, 4, 64))
    m["bsg"] = np.broadcast_to(f("l0_b_subln")[None, :], (128, 128))
    m["wout0"] = lay_rows(f("l0_w_out"))
    m.update(mem_inputs(inp, 0))
    m["wgu_l0f2"] = lay_wgu(f("l0_ffn2_wg"), f("l0_ffn2_wu"))
    m["wd_l0f2"] = f("l0_ffn2_wd")
    m["wgu_l1f1"] = lay_wgu(f("l1_ffn1_wg"), f("l1_ffn1_wu"))
    m["wd_l1f1"] = f("l1_ffn1_wd")
    m["hg1"] = np.stack([tile2(f("l1_c_q_norm")), tile2(f("l1_c_k_norm"))], axis=1)
    w = f("l1_w_in")
    m["w_fm1"] = lay_cols(w[:, :2048])
    m["w_tm1"] = lay_k(w[:, 2048:])
    return {k: np.ascontiguousarray(v) for k, v in m.items()}


NCORES = 8
SEQ = 16384
TOKC = SEQ // NCORES


def build_stage(which):
    nc = bass.Bass("TRN2", target_bir_lowering=False)
    D = {1: stage1_dram, 2: stage2_dram, 3: stage3_dram}[which](nc, TOKC)
    with ExitStack() as es:
        C = Ctx(nc, es, TOKC)
        {1: stage1_body, 2: stage2_body, 3: stage3_body}[which](C, D)
        finish_all(C)
    return nc


def interleave_fm(parts):
    a = np.stack(parts, axis=0)
    lead = a.shape[1:-1]
    a = a.reshape((NCORES,) + lead + (16, 128))
    a = np.moveaxis(a, 0, -2)
    return np.ascontiguousarray(a.reshape(lead + (SEQ,)))


def interleave_tm(parts):
    a = np.stack(parts, axis=0).reshape(NCORES, 16, 128, -1)
    return np.ascontiguousarray(a.transpose(1, 0, 2, 3).reshape(SEQ, -1))


def kernel(**inputs):
    inp = {k: np.asarray(v) for k, v in inputs.items()}
    cores = list(range(NCORES))
    nc1 = build_stage(1)
    ims = []
    for c in cores:
        tok_idx = (np.arange(16)[:, None] * 1024 + 128 * c + np.arange(128)[None, :]).reshape(-1)
        ims.append(stage1_inputs(inp, tok_idx))
    r1 = run_bass_kernel_spmd(nc1, ims, core_ids=cores).results
    ks = interleave_fm([r1[c]["ksT"] for c in cores])
    g = {"akT_g": np.ascontiguousarray(ks[0:4]), "bkT_g": np.ascontiguousarray(ks[4:8]),
         "ikT_g": interleave_fm([r1[c]["ikT"] for c in cores]),
         "vA_g": interleave_tm([r1[c]["vA"] for c in cores]),
         "vB_g": interleave_tm([r1[c]["vB"] for c in cores])}
    W2 = stage2_weights(inp)
    t5 = np.asarray(inp["t5_bias"], np.float32)
    nc2 = build_stage(2)
    ims = []
    for c in cores:
        m = dict(W2)
        m.update(g)
        m["hT1"] = r1[c]["hT1"]; m["qside"] = r1[c]["qside"]; m["sgn"] = r1[c]["sgn"]
        m["nbias"] = near_bias_host(t5, c)
        m["adm"] = adm_host(c)
        ims.append(m)
    r2 = run_bass_kernel_spmd(nc2, ims, core_ids=cores).results
    hT2 = interleave_fm([r2[c]["hT2"] for c in cores])
    q1 = interleave_fm([r2[c]["q1"] for c in cores])
    k1 = interleave_fm([r2[c]["k1"] for c in cores])
    v1 = interleave_tm([r2[c]["v1"] for c in cores])
    k1p = np.concatenate([np.zeros((8, 128, 512), k1.dtype), k1], axis=2)
    v1p = np.concatenate([np.zeros((512, 1040), v1.dtype), v1], axis=0)
    W3 = stage3_weights(inp)
    nc3 = build_stage(3)
    ims = []
    for c in cores:
        t0 = TOKC * c
        m = dict(W3)
        m["hT2"] = np.ascontiguousarray(hT2[:, t0:t0 + TOKC])
        m["q1"] = np.ascontiguousarray(q1[:, :, t0:t0 + TOKC])
        m["kband"] = np.ascontiguousarray(k1p[:, :, t0:t0 + TOKC + 512])
        m["vband"] = np.ascontiguousarray(v1p[t0:t0 + TOKC + 512])
        ims.append(m)
    r3 = run_bass_kernel_spmd(nc3, ims, core_ids=cores).results
    out = np.concatenate([r3[c]["outT"].T for c in cores], axis=0)
    return np.ascontiguousarray(out.reshape(1, SEQ, 1024).astype(np.float32))
```
